# Optimizing a Trainium2 kernel written in Bass

```python
import jax, jax.numpy as jnp
from jax import lax
import numpy as np

D_MODEL = 1024
BATCH = 8
SEQ = 2048
DEPTH = 4

CTX_LEN = 256
GRID_W = 64
F_WIDTH = D_MODEL // 2
F_GROUPS = 4
F_GROUP_DIM = F_WIDTH // F_GROUPS
HEAD_DIM = 64
NA_WIDTH = D_MODEL - F_WIDTH
NA_HEADS = NA_WIDTH // HEAD_DIM
MIX_WIDTH = F_WIDTH + NA_WIDTH
IN_COLS = 2 * F_WIDTH + 4 * NA_WIDTH
WIN_ROWS_MAX = 8
WIN_COLS = 16
EPS = 1e-6
NEG_INF = -1e30

kernel_name = "hybrid_fourier_natten_dit_block"


def _rms_norm(t, g):
    t32 = t.astype(jnp.float32)
    y = t32 * lax.rsqrt(jnp.mean(t32 * t32, axis=-1, keepdims=True) + EPS)
    return (y * g.astype(jnp.float32)).astype(t.dtype)


def _ada(cond, w_ada, b_ada):
    mod = jax.nn.silu(cond) @ w_ada + b_ada
    return jnp.split(mod, 3, axis=-1)


def _split_proj(p):
    f_in = p[..., :F_WIDTH]
    f_gate = p[..., F_WIDTH:2 * F_WIDTH]
    q, k, v, na_gate = jnp.split(p[..., 2 * F_WIDTH:], 4, axis=-1)
    return f_in, f_gate, q, k, v, na_gate


def _heads(t):
    return t.reshape(t.shape[0], t.shape[1], NA_HEADS, HEAD_DIM)


def _fourier(u, w_four):
    b, l, _ = u.shape
    ug = u.reshape(b, l, F_GROUPS, F_GROUP_DIM).astype(jnp.float32)
    y = jnp.real(jnp.fft.fft2(ug, axes=(1, 3))) * ((l * F_GROUP_DIM) ** -0.5)
    y = y.reshape(b, l, F_WIDTH).astype(u.dtype)
    return y @ w_four


def _ctx_attention(qc, kc, vc):
    s = jnp.einsum('bqhd,bkhd->bhqk', qc, kc, preferred_element_type=jnp.float32) * (HEAD_DIM ** -0.5)
    p = jax.nn.softmax(s, axis=-1).astype(vc.dtype)
    o = jnp.einsum('bhqk,bkhd->bqhd', p, vc)
    return o.reshape(o.shape[0], o.shape[1], NA_WIDTH)


def _neighbourhood_attention(q, k, v, kc, vc, rel_bias):
    b, l, h, d = q.shape
    rows = l // GRID_W
    wr = min(WIN_ROWS_MAX, rows)
    r_ar = jnp.arange(rows)
    rs = jnp.clip(r_ar - wr // 2, 0, rows - wr)
    row_idx = rs[:, None] + jnp.arange(wr)[None, :]
    col_ar = jnp.arange(GRID_W)
    cs = jnp.clip(col_ar - WIN_COLS // 2, 0, GRID_W - WIN_COLS)
    in_win = (col_ar[None, :] >= cs[:, None]) & (col_ar[None, :] < cs[:, None] + WIN_COLS)
    mask = jnp.broadcast_to(in_win[:, None, :], (GRID_W, wr, GRID_W)).reshape(GRID_W, wr * GRID_W)
    dr_idx = row_idx - r_ar[:, None] + (WIN_ROWS_MAX - 1)
    dc_idx = jnp.clip(col_ar[None, :] - col_ar[:, None], -(WIN_COLS - 1), WIN_COLS - 1) + (WIN_COLS - 1)
    bias = rel_bias[:, dr_idx[:, None, :, None], dc_idx[None, :, None, :]]
    bias = bias.reshape(h, rows, GRID_W, wr * GRID_W).astype(jnp.float32)
    qg = q.reshape(b, rows, GRID_W, h, d)
    k_rows = k.reshape(b, rows, GRID_W, h, d)[:, row_idx].reshape(b, rows, wr * GRID_W, h, d)
    v_rows = v.reshape(b, rows, GRID_W, h, d)[:, row_idx].reshape(b, rows, wr * GRID_W, h, d)
    scale = HEAD_DIM ** -0.5
    s_loc = jnp.einsum('brqhd,brkhd->bhrqk', qg, k_rows, preferred_element_type=jnp.float32) * scale
    s_loc = jnp.where(mask[None, None, None], s_loc + bias[None], NEG_INF)
    s_ctx = jnp.einsum('brqhd,bkhd->bhrqk', qg, kc, preferred_element_type=jnp.float32) * scale
    p = jax.nn.softmax(jnp.concatenate([s_loc, s_ctx], axis=-1), axis=-1).astype(v.dtype)
    n_loc = wr * GRID_W
    o = (jnp.einsum('bhrqk,brkhd->brqhd', p[..., :n_loc], v_rows)
         + jnp.einsum('bhrqk,bkhd->brqhd', p[..., n_loc:], vc))
    return o.reshape(b, l, NA_WIDTH)


def setup_inputs(seed: int = 0) -> dict:
    key = jax.random.key(seed)
    ks = jax.random.split(key, 14)
    f32 = jnp.float32
    n = lambda k, shape: jax.random.normal(k, shape, f32)
    return {
        "x": n(ks[0], (BATCH, SEQ, D_MODEL)),
        "c": n(ks[1], (BATCH, D_MODEL)),
        "ctx": n(ks[2], (BATCH, CTX_LEN, D_MODEL)),
        "c_ctx": n(ks[3], (D_MODEL,)),
        "norm_g": 1.0 + 0.02 * n(ks[4], (DEPTH, D_MODEL)),
        "w_ada": n(ks[5], (DEPTH, D_MODEL, 3 * D_MODEL)) * D_MODEL ** -0.5,
        "b_ada": 0.02 * n(ks[6], (DEPTH, 3 * D_MODEL)),
        "w_in": n(ks[7], (DEPTH, D_MODEL, IN_COLS)) * D_MODEL ** -0.5,
        "w_four": n(ks[8], (DEPTH, F_WIDTH, F_WIDTH)) * F_WIDTH ** -0.5,
        "q_norm_g": 1.0 + 0.02 * n(ks[9], (DEPTH, HEAD_DIM)),
        "k_norm_g": 1.0 + 0.02 * n(ks[10], (DEPTH, HEAD_DIM)),
        "rel_bias": 0.02 * n(ks[11], (DEPTH, NA_HEADS, 2 * WIN_ROWS_MAX - 1, 2 * WIN_COLS - 1)),
        "w_out": n(ks[12], (DEPTH, MIX_WIDTH, D_MODEL)) * MIX_WIDTH ** -0.5,
    }


def reference(x, c, ctx, c_ctx, norm_g, w_ada, b_ada, w_in, w_four, q_norm_g, k_norm_g, rel_bias, w_out):
    for l in range(DEPTH):
        last = l == DEPTH - 1
        sh_x, sc_x, g_x = [t[:, None, :] for t in _ada(c, w_ada[l], b_ada[l])]
        sh_c, sc_c, g_c = _ada(c_ctx, w_ada[l], b_ada[l])
        xn = _rms_norm(x, norm_g[l]) * (1.0 + sc_x) + sh_x
        cn = _rms_norm(ctx, norm_g[l]) * (1.0 + sc_c) + sh_c
        fx_in, fx_gate, qx, kx, vx, nx_gate = _split_proj(xn @ w_in[l])
        fc_in, fc_gate, qc, kc, vc, nc_gate = _split_proj(cn @ w_in[l])
        qx = _rms_norm(_heads(qx), q_norm_g[l])
        kx = _rms_norm(_heads(kx), k_norm_g[l])
        vx = _heads(vx)
        qc = _rms_norm(_heads(qc), q_norm_g[l])
        kc = _rms_norm(_heads(kc), k_norm_g[l])
        vc = _heads(vc)
        f_out = _fourier(fx_in, w_four[l]) * jax.nn.silu(fx_gate)
        na_out = _neighbourhood_attention(qx, kx, vx, kc, vc, rel_bias[l]) * jax.nn.silu(nx_gate)
        x_new = x + g_x * (jnp.concatenate([f_out, na_out], axis=-1) @ w_out[l])
        if not last:
            fc_out = _fourier(fc_in, w_four[l]) * jax.nn.silu(fc_gate)
            nc_out = _ctx_attention(qc, kc, vc) * jax.nn.silu(nc_gate)
            ctx = ctx + g_c * (jnp.concatenate([fc_out, nc_out], axis=-1) @ w_out[l])
        x = x_new
    return x
```

```python
import contextlib
import numpy as np
import ml_dtypes
import concourse.bass as bass
import concourse.mybir as mybir
from concourse.bass_utils import run_bass_kernel_spmd

F32, BF16 = mybir.dt.float32, mybir.dt.bfloat16
AF = mybir.ActivationFunctionType
ALU = mybir.AluOpType

DEPTH = 4
DM = 1024
SEQ = 2048
CTX = 256
NT = SEQ + CTX
BLOCKS = [(0, 512), (512, 512), (1024, 512), (1536, 512), (2048, 256)]
EPS = 1e-6
NEG = -30000.0
ENGS = ("pe", "act", "dve", "pool", "sp")

RS = [min(max(r - 4, 0), 24) for r in range(32)]


def tile_chunks(t):
    lo = RS[2 * t] // 2
    hi = (RS[2 * t + 1] + 7) // 2
    return list(range(lo, hi + 1))


def chunk_allowed(t, c):
    return tuple(
        tuple(RS[2 * t + qp] <= 2 * c + kp <= RS[2 * t + qp] + 7 for qp in (0, 1)) for kp in (0, 1)
    )


def build_variants():
    var = {}
    cmap = {}
    for t in range(16):
        for c in tile_chunks(t):
            key = (c - t, chunk_allowed(t, c))
            if key not in var:
                var[key] = len(var)
            cmap[(t, c)] = var[key]
    return var, cmap


VARIANTS, CHUNK_VAR = build_variants()
NV = len(VARIANTS)


class Buf:
    __slots__ = ("name", "w", "r")

    def __init__(self, name=""):
        self.name = name
        self.w = None
        self.r = []


class Sched:
    def __init__(self, nc, n_dma):
        self.nc = nc
        self.q = {e: [] for e in ENGS}
        self.n_dma = n_dma
        self.dma_count = [0] * n_dma

    def _deps(self, reads, writes, extra):
        waits = list(extra)
        for b in reads:
            if b.w is not None:
                waits.append(b.w)
        for b in writes:
            if b.w is not None:
                waits.append(b.w)
            waits.extend(b.r)
        return waits

    def _commit(self, tok, reads, writes):
        for b in reads:
            b.r.append(tok)
        for b in writes:
            b.w = tok
            b.r = []

    def op(self, eng, fn, reads=(), writes=(), extra=()):
        waits = self._deps(reads, writes, extra)
        tok = ("e", eng, len(self.q[eng]))
        self.q[eng].append([fn, waits, False, None])
        self._commit(tok, reads, writes)
        return tok

    def dma(self, fn, slot, reads=(), writes=(), eng="sp", extra=()):
        waits = self._deps(reads, writes, extra)
        self.dma_count[slot] += 16
        tok = ("d", slot, self.dma_count[slot])
        self.q[eng].append([fn, waits, False, slot])
        self._commit(tok, reads, writes)
        return tok

    def finalize(self, final_waits=()):
        nc = self.nc
        for e in ENGS:
            for item in self.q[e]:
                for t in item[1]:
                    if t[0] == "e":
                        self.q[t[1]][t[2]][2] = True
        for t in final_waits:
            if t[0] == "e":
                self.q[t[1]][t[2]][2] = True
        cum = {}
        for e in ENGS:
            c = 0
            arr = []
            for item in self.q[e]:
                if item[2]:
                    c += 1
                arr.append(c)
            cum[e] = arr
        with contextlib.ExitStack() as st:
            esem = {e: st.enter_context(nc.semaphore("s_" + e)) for e in ENGS}
            dsem = [st.enter_context(nc.semaphore("d_%d" % i)) for i in range(self.n_dma)]
            block = st.enter_context(nc.Block())

            def resolve(t):
                if t[0] == "e":
                    return ("e", t[1]), esem[t[1]], cum[t[1]][t[2]]
                return ("d", t[1]), dsem[t[1]], t[2]

            def run(e, engobj, tail=()):
                waited = {}

                def do_waits(waits):
                    need = {}
                    for t in waits:
                        key, sem, val = resolve(t)
                        if key == ("e", "pe") and e == "pe":
                            continue
                        if waited.get(key, 0) >= val:
                            continue
                        if need.get(key, (None, 0))[1] < val:
                            need[key] = (sem, val)
                    for key, (sem, val) in need.items():
                        engobj.wait_ge(sem, val)
                        waited[key] = val

                for fn, waits, sig, dslot in self.q[e]:
                    do_waits(waits)
                    ins = fn(engobj)
                    if dslot is not None:
                        ins.then_inc(dsem[dslot], 16)
                    elif sig:
                        ins.then_inc(esem[e], 1)
                do_waits(tail)

            @block.tensor
            def _(eng):
                run("pe", eng)

            @block.scalar
            def _(eng):
                run("act", eng)

            @block.vector
            def _(eng):
                run("dve", eng)

            @block.gpsimd
            def _(eng):
                run("pool", eng)

            @block.sync
            def _(eng):
                run("sp", eng, tail=final_waits)


def build_nc(layers=tuple(range(DEPTH)), final_layer=DEPTH - 1, stop_after=None, dbg=()):
    nc = bass.Bass("TRN2", target_bir_lowering=False)
    nL = DEPTH

    def din(name, shape, dt=F32):
        return nc.dram_tensor(name, list(shape), dt, kind="ExternalInput").ap()

    xT_d = din("xT", [128, 8, NT])
    cT_d = din("cT", [128, 8, 2])
    bada_d = din("bada", [128, nL * 48])
    ng2_d = din("ng2", [128, nL * 16])
    qg_d = din("qg", [128, nL])
    kg_d = din("kg", [128, nL])
    wada_d = din("wada", [nL, 24, 128, 8, 128])
    win_d = din("win", [nL, 24, 128, 8, 128])
    w4_d = din("w4", [nL, 128, 4, 512])
    wo_d = din("wo", [nL, 2, 8, 128, 4, 128])
    bm_d = din("bm", [nL, 4, 128, NV, 256])
    cs_d = din("cs", [16, 128, 2, SEQ], BF16)
    csc_d = din("csc", [128, 2, 2, CTX], BF16)
    cs128_d = din("cs128", [128, 256], BF16)
    ident_d = din("ident", [128, 128], BF16)
    oT_d = nc.dram_tensor("oT", [128, 8, SEQ], F32, kind="ExternalOutput").ap()

    off = [16512]

    def alloc(name, shape, dt, at=None):
        nbytes = int(np.prod(shape[1:])) * (4 if dt == F32 else 2)
        nbytes = (nbytes + 31) // 32 * 32
        if at is None:
            o = off[0]
            off[0] += nbytes
        else:
            o = at
        return nc.alloc_sbuf_tensor_at(name, list(shape), dt, offset=o), o + nbytes

    NWST = 4
    NWOS = 4
    NCST = 3
    XT, _ = alloc("XT", [128, 8, NT], F32)
    xnT, _ = alloc("xnT", [128, 8, NT], BF16)
    r1 = off[0]
    AB, _ = alloc("AB", [128, 18, 1024], BF16, at=r1)
    kT, e1 = alloc("kT", [128, 4, NT], BF16, at=r1)
    Vp, e2 = alloc("Vp", [128, 18, 4, 192], BF16, at=e1)
    off[0] = e2
    wst, _ = alloc("wst", [128, NWST, 8, 128], BF16)
    wos, _ = alloc("wos", [128, NWOS, 4, 128], BF16)
    bmt, _ = alloc("bmt", [128, 2, NV, 256], BF16)
    ident, _ = alloc("ident", [128, 128], BF16)
    ones, _ = alloc("ones", [128, 128], BF16)
    bd, _ = alloc("bd", [128, 128], BF16)
    cs128, _ = alloc("cs128", [128, 256], BF16)
    cT, _ = alloc("cT", [128, 8, 2], F32)
    cTs, _ = alloc("cTs", [128, 8, 2], BF16)
    bada, _ = alloc("bada", [128, nL * 48], F32)
    ng2, _ = alloc("ng2", [128, nL * 16], F32)
    qg, _ = alloc("qg", [128, nL], F32)
    kg, _ = alloc("kg", [128, nL], F32)
    qg8, _ = alloc("qg8", [128, nL], F32)
    modT, _ = alloc("modT", [128, nL * 48], F32)
    gsT, _ = alloc("gsT", [128, nL * 16], F32)
    epsT, _ = alloc("epsT", [128, 1], F32)
    rstdN, _ = alloc("rstdN", [128, 512], F32)
    y0, _ = alloc("y0", [128, 16], BF16)
    c0v, _ = alloc("c0v", [128, 16], BF16)
    S0 = off[0]
    sq, e = alloc("sq", [128, 4, 512], BF16, at=S0)
    tmpA, e = alloc("tmpA", [128, 512], F32, at=e)
    rstd, e = alloc("rstd", [128, 512], F32, at=e)
    S1 = e
    tmpB, eN = alloc("tmpB", [128, 2, 512], F32, at=e)
    cst, e = alloc("cst", [128, NCST, 2, 512], BF16, at=S0)
    wac = [alloc("wac%d" % k_, [128, 8, 128], BF16, at=S0 + k_ * 2048)[0] for k_ in range(NCST)]
    wa0s = [alloc("wa0s%d" % k_, [128, 8, 128], BF16, at=r1 + k_ * 2048)[0] for k_ in range(8)]
    YT, e = alloc("YT", [128, 4, 512], BF16, at=e)
    YTm, e = alloc("YTm", [128, 4, 512], BF16, at=e)
    mixF, e = alloc("mixF", [128, 4, 512], BF16, at=e)
    fg_off = e
    fg, e = alloc("fg", [128, 2, 512], BF16, at=e)
    fin_off = e
    finT, e = alloc("finT", [128, 2, 512], BF16, at=e)
    Qs0, _ = alloc("Qs0", [128, 512], F32, at=fg_off)
    Qs1, _ = alloc("Qs1", [128, 512], F32, at=fin_off)
    w4s, e = alloc("w4s", [128, 4, 512], BF16, at=e)
    wa_off = e
    wa, eF = alloc("wa", [128, 1, 8, 128], BF16, at=e)
    wosx, _ = alloc("wosx", [128, 2, 4, 128], BF16, at=wa_off)
    wv, eA1 = alloc("wv", [128, 8, 512], BF16, at=S1)
    qz, e = alloc("qz", [128, 4, 4, 2, 128], BF16, at=S1)
    ngT, e = alloc("ngT", [128, 4, 512], BF16, at=e)
    mixA, e = alloc("mixA", [128, 4, 512], BF16, at=e)
    PT, eA2 = alloc("PT", [128, 2, 1024], BF16, at=e)
    lnd, wgt = tmpA, rstd
    tmpB2, _ = alloc("tmpB2", [128, 512], F32, at=S0 + 2048)
    top = max(eN, eF, eA1, eA2)
    assert top <= 229344, top

    st = contextlib.ExitStack()
    psall = st.enter_context(nc.psum_tensor("psall", [128, 4096], F32))
    PS = [psall[:, i * 512:(i + 1) * 512] for i in range(8)]

    slot_names = (["a%d" % i for i in range(8)] + ["const", "x0", "x1", "x2", "x3", "x4", "wa0", "wa1", "bmt0", "bmt1", "w4", "wv0", "wv1", "wv2", "wv3", "out"]
                  + ["wst%d" % i for i in range(NWST)] + ["wos%d" % i for i in range(NWOS)]
                  + ["cst%d" % i for i in range(NCST)])
    SL = {n: i for i, n in enumerate(slot_names)}
    s = Sched(nc, len(slot_names))

    B_XT = [Buf("XT%d" % b) for b in range(5)]
    B_xn = [Buf("xn%d" % b) for b in range(5)]
    B_P = [Buf("P%d" % i) for i in range(8)]
    B_const = Buf("const")
    B_wst = [Buf("wst%d" % i) for i in range(NWST)]
    B_wos = [Buf("wos%d" % i) for i in range(NWOS)]
    B_bmt = [Buf("bmt0"), Buf("bmt1")]
    B_mod = [Buf("mod%d" % l) for l in range(nL)]
    B_cTs = Buf("cTs")
    B_qg8 = Buf("qg8")
    st_scratch = {"bufs": []}
    r1_state = {"bufs": []}

    def new_phase_bufs(state, names):
        toks = []
        for b in state["bufs"]:
            if b.w is not None:
                toks.append(b.w)
            toks.extend(b.r)
        out = []
        for n in names:
            b = Buf(n)
            b.r = list(toks)
            out.append(b)
        state["bufs"] = out
        return out

    cnt = {"wst": 0, "wos": 0, "bmt": 0, "cst": 0, "wa": 0, "pp": 0, "wosF": 0}

    def mm(out, lhsT, rhs, start, stop, reads, writes):
        return s.op("pe", lambda e: e.matmul(out, lhsT=lhsT, rhs=rhs, start=start, stop=stop), reads, writes)

    def act(out, in_, func, reads, writes, bias=None, scale=None):
        kw = {}
        if bias is not None:
            kw["bias"] = bias
        if scale is not None:
            kw["scale"] = scale
        return s.op("act", lambda e: e.activation(out=out, in_=in_, func=func, **kw), reads, writes)

    def dve_tt(out, in0, in1, op, reads, writes):
        return s.op("dve", lambda e: e.tensor_tensor(out=out, in0=in0, in1=in1, op=op), reads, writes)

    def dve_stt(out, in0, scalar, in1, op0, op1, reads, writes):
        return s.op("dve", lambda e: e.scalar_tensor_tensor(out=out, in0=in0, scalar=scalar, in1=in1,
                                                            op0=op0, op1=op1), reads, writes)

    def dve_ts(out, in0, scalar1, op0, reads, writes):
        return s.op("dve", lambda e: e.tensor_scalar(out=out, in0=in0, scalar1=scalar1, scalar2=None, op0=op0),
                    reads, writes)

    def dve_copy(out, in_, reads, writes):
        return s.op("dve", lambda e: e.tensor_copy(out=out, in_=in_), reads, writes)

    def dma(out, in_, slot, reads=(), writes=(), eng="sp"):
        return s.dma(lambda e: e.dma_start(out=out, in_=in_), slot, reads=reads, writes=writes, eng=eng)

    def load_w(fc, l):
        k = cnt["wst"] % NWST
        cnt["wst"] += 1
        dma(wst[:, k], win_d[l, fc], SL["wst%d" % k], writes=[B_wst[k]], eng="pool")
        return k

    cd = SL["const"]
    for dst, src in ((ident[:], ident_d[:, :]), (cs128[:], cs128_d[:, :]),
                     (cT[:], cT_d[:, :, :]), (bada[:], bada_d[:, :]), (ng2[:], ng2_d[:, :]),
                     (qg[:], qg_d[:, :]), (kg[:], kg_d[:, :])):
        dma(dst, src, cd)
    const_tok = ("d", cd, s.dma_count[cd])
    B_const.w = const_tok
    for b, (t0, n) in enumerate(BLOCKS):
        dma(XT[:, :, t0:t0 + n], xT_d[:, :, t0:t0 + n], SL["x%d" % b], writes=[B_XT[b]])
    B_misc = Buf("misc")
    s.op("pool", lambda e: e.memset(ones[:], 1.0), writes=[B_misc])
    s.op("pool", lambda e: e.memset(bd[:], 0.0), writes=[B_misc])
    s.op("pool", lambda e: e.memset(bd[0:64, 0:64], 1.0), writes=[B_misc])
    s.op("pool", lambda e: e.memset(bd[64:128, 64:128], 1.0), writes=[B_misc])
    s.op("pool", lambda e: e.memset(epsT[:], EPS), writes=[B_misc])
    s.op("pool", lambda e: e.memset(c0v[:], 1.0 / 512.0), writes=[B_misc])
    act(cTs[:], cT[:], AF.Silu, [B_const], [B_cTs])
    dve_ts(qg8[:], qg[:], 0.125, ALU.mult, [B_const], [B_qg8])

    ada_state = {}
    stored = set()

    def ada_piece(l, fcg, b_wa, bank=0, stage=None):
        if stage is None:
            wt, bw, sl = wa[:, 0], b_wa[0], SL["wa0"]
        else:
            wt, bw, sl = stage
        dma(wt, wada_d[l, fcg], sl, writes=[bw], eng="pool")
        for kc in range(8):
            mm(PS[bank][:, 0:2], wt[:, kc, :], cTs[:, kc, :], kc == 0, kc == 7, [bw, B_cTs], [B_P[bank]])
        c0 = l * 48 + fcg * 2
        dve_tt(modT[:, c0:c0 + 2], PS[bank][:, 0:2], bada[:, c0:c0 + 2], ALU.add, [B_P[bank], B_const], [B_mod[l]])

    def ada_finish(l):
        dve_ts(gsT[:, l * 16:(l + 1) * 16], modT[:, l * 48 + 16:l * 48 + 32], 1.0, ALU.add, [B_mod[l]], [B_mod[l]])
        dve_tt(gsT[:, l * 16:(l + 1) * 16], gsT[:, l * 16:(l + 1) * 16], ng2[:, l * 16:(l + 1) * 16], ALU.mult,
               [B_mod[l], B_const], [B_mod[l]])

    def phase_ada_standalone(l):
        b_wa = new_phase_bufs(st_scratch, ["wa0", "wa1"])
        for fcg in range(24):
            ada_piece(l, fcg, b_wa)
        ada_finish(l)

    def rstd_from_ss(ps, n, inv_n, b_ps, b_tmpA, b_rstd):
        act(tmpA[:, :n], ps[:, :n], AF.Ln, [b_ps, B_misc], [b_tmpA], bias=epsT[:, 0:1], scale=inv_n)
        act(rstd[:, :n], tmpA[:, :n], AF.Exp, [b_tmpA], [b_rstd], scale=-0.5)

    def phase_norm(l, with_ada=False):
        names = ["sq0", "sq1", "sq2", "sq3", "tmpA", "rstd", "tmpB0", "tmpB1", "wa0", "wa1"]
        bs = new_phase_bufs(st_scratch, names)
        b_sq, b_tmpA, b_rstd, b_tmpB, b_wa = bs[0:4], bs[4], bs[5], bs[6:8], bs[8:10]
        ada_todo = list(range(24)) if with_ada else []
        b_a0 = [Buf("a0_%d" % k_) for k_ in range(8)]
        na = [0]

        def stage0():
            k_ = na[0] % 8
            na[0] += 1
            return (wa0s[k_][:], b_a0[k_], SL["a%d" % k_])

        i = 0
        for b, (t0, n) in enumerate(BLOCKS):
            for kc in range(8):
                k = i % 4
                i += 1
                act(sq[:, k, :n], XT[:, kc, t0:t0 + n], AF.Square, [B_XT[b]], [b_sq[k]])
                mm(PS[1 + b][:, :n], ones[:], sq[:, k, :n], kc == 0, kc == 7, [b_sq[k], B_misc], [B_P[1 + b]])
            for _ in range(5):
                if ada_todo:
                    ada_piece(l, ada_todo.pop(0), b_wa, stage=stage0())
        while ada_todo:
            ada_piece(l, ada_todo.pop(0), b_wa, stage=stage0())
        r1_state["bufs"] = r1_state["bufs"] + b_a0
        if with_ada:
            ada_finish(l)
        for b, (t0, n) in enumerate(BLOCKS):
            sidx = 0 if b < 4 else 1
            rstd_from_ss(PS[1 + b], n, 1.0 / DM, B_P[1 + b], b_tmpA, b_rstd)
            for kc in range(8):
                k = kc % 2
                g_ap = gsT[:, l * 16 + kc * 2 + sidx:l * 16 + kc * 2 + sidx + 1]
                sh_ap = modT[:, l * 48 + kc * 2 + sidx:l * 48 + kc * 2 + sidx + 1]
                dve_stt(tmpB[:, k, :n], XT[:, kc, t0:t0 + n], g_ap, rstd[:, :n], ALU.mult, ALU.mult,
                        [B_XT[b], b_rstd, B_mod[l]], [b_tmpB[k]])
                act(xnT[:, kc, t0:t0 + n], tmpB[:, k, :n], AF.Identity, [b_tmpB[k], B_mod[l]], [B_xn[b]], bias=sh_ap)

    def proj(fc_slot, b, ps, b_ps):
        t0, n = BLOCKS[b]
        for kc in range(8):
            mm(ps[:, :n], wst[:, fc_slot, kc, :], xnT[:, kc, t0:t0 + n], kc == 0, kc == 7,
               [B_wst[fc_slot], B_xn[b]], [b_ps])

    def next_pp():
        k = cnt["pp"] % 2
        cnt["pp"] += 1
        return k

    def phase_fourier(l, last, ada_next):
        names = (["cst%d" % i for i in range(NCST)] + ["YT", "mixF", "fg0", "fg1", "finT0", "finT1", "w4s", "wa0", "wa1", "YTm"])
        bs = new_phase_bufs(st_scratch, names)
        b_cst = bs[0:NCST]
        b_YT, b_mixF, b_fg, b_fin, b_w4, b_wa = bs[NCST], bs[NCST + 1], bs[NCST + 2:NCST + 4], bs[NCST + 4:NCST + 6], bs[NCST + 6], bs[NCST + 7:NCST + 9]
        b_YTm = bs[NCST + 9]
        b_AB = new_phase_bufs(r1_state, ["AB%d" % j for j in range(18)])
        f1_slots = {0: load_w(0, l)}
        dma(w4s[:], w4_d[l], SL["w4"], writes=[b_w4], eng="pool")
        i2 = [0]

        def chdft(g, b, fk):
            t0, n = BLOCKS[b]
            for jl in range(n // 128):
                jt = t0 // 128 + jl
                pb = 2 + (i2[0] % 4)
                i2[0] += 1
                mm(PS[pb][:, 0:256], finT[:, fk, jl * 128:(jl + 1) * 128], cs128[:, :], True, True,
                   [b_fin[fk], B_const], [B_P[pb]])
                dve_copy(AB[:, jt, g * 256:(g + 1) * 256], PS[pb][:, 0:256], [B_P[pb]], [b_AB[jt]])

        pend = None
        for g in range(4):
            k = f1_slots[g] if g in f1_slots else load_w(g, l)
            for b, (t0, n) in enumerate(BLOCKS):
                pp = next_pp()
                proj(k, b, PS[pp], B_P[pp])
                fk = (g * 5 + b) % 2
                act(finT[:, fk, :n], PS[pp][:, :n], AF.Copy, [B_P[pp]], [b_fin[fk]])
                if pend is not None:
                    chdft(*pend)
                pend = (g, b, fk)
        chdft(*pend)
        ada_todo = list(range(24)) if ada_next is not None else []

        def next_stage():
            k_ = cnt["cst"] % NCST
            cnt["cst"] += 1
            return (wac[k_][:], b_cst[k_], SL["cst%d" % k_])

        def xb(lst, t0, n):
            return [lst[i] for i in sorted({t0 // 512, (t0 + n - 1) // 512})]

        def f3(Ysrc, b_Y, ranges, sidx):
            ntot = sum(r[1] for r in ranges)
            for oc in range(4):
                k = load_w(4 + oc, l)
                fk = oc % 2
                gb, zb = fk, 6 + fk
                for (t0, n, c0) in ranges:
                    for kc in range(8):
                        mm(PS[gb][:, c0:c0 + n], wst[:, k, kc, :], xnT[:, kc, t0:t0 + n], kc == 0, kc == 7,
                           [B_wst[k]] + xb(B_xn, t0, n), [B_P[gb]])
                act(fg[:, fk, :ntot], PS[gb][:, :ntot], AF.Silu, [B_P[gb]], [b_fg[fk]])
                for k4 in range(4):
                    mm(PS[zb][:, :ntot], w4s[:, k4, oc * 128:(oc + 1) * 128], Ysrc[:, k4, :ntot], k4 == 0, k4 == 3,
                       [b_w4, b_Y], [B_P[zb]])
                dve_tt(mixF[:, oc, :ntot], PS[zb][:, :ntot], fg[:, fk, :ntot], ALU.mult, [B_P[zb], b_fg[fk]], [b_mixF])
                if ada_todo:
                    ada_piece(ada_next, ada_todo.pop(0), b_wa, bank=2, stage=next_stage())
            banks = (0, 1, 6, 7)
            ring = [(wos[:, j_], B_wos[j_], SL["wos%d" % j_]) for j_ in range(NWOS)] + \
                   [(wosx[:, j_], b_wa[j_], SL["wa%d" % j_]) for j_ in range(2)]
            for dc in range(8):
                wt, bw, sl = ring[cnt["wosF"] % len(ring)]
                cnt["wosF"] += 1
                dma(wt, wo_d[l, 0, dc], sl, writes=[bw], eng="pool")
                pb = banks[dc % 4]
                for k4 in range(4):
                    mm(PS[pb][:, :ntot], wt[:, k4, :], mixF[:, k4, :ntot], k4 == 0, k4 == 3, [bw, b_mixF], [B_P[pb]])
                g_ap = modT[:, l * 48 + (16 + dc) * 2 + sidx:l * 48 + (16 + dc) * 2 + sidx + 1]
                for (t0, n, c0) in ranges:
                    dve_stt(XT[:, dc, t0:t0 + n], PS[pb][:, c0:c0 + n], g_ap, XT[:, dc, t0:t0 + n], ALU.mult, ALU.add,
                            [B_P[pb], B_mod[l]] + xb(B_XT, t0, n), xb(B_XT, t0, n))
                if ada_todo and dc in (2, 5):
                    ada_piece(ada_next, ada_todo.pop(0), b_wa, bank=2, stage=next_stage())

        for c in range(4):
            for jt in range(16):
                mm(PS[7][:, c:c + 1], AB[:, jt, c * 256:c * 256 + 128], c0v[:, 0:1], jt == 0, jt == 15,
                   [b_AB[jt], B_misc], [B_P[7]])
        b_y0 = Buf("y0")
        dve_copy(y0[:, 0:4], PS[7][:, 0:4], [B_P[7]], [b_y0])
        Qs = (Qs0, Qs1)
        b_Qs = ([b_fg[0], b_fg[1]], [b_fin[0], b_fin[1]])
        for u in range(2):
            kc0 = 1 + 512 * u
            for jt in range(16):
                k = cnt["cst"] % NCST
                cnt["cst"] += 1
                dma(cst[:, k], cs_d[jt, :, :, kc0:kc0 + 512], SL["cst%d" % k], writes=[b_cst[k]])
                for c in range(4):
                    mm(PS[2 * c][:, :], AB[:, jt, c * 256:c * 256 + 128], cst[:, k, 0, :], jt == 0, jt == 15,
                       [b_AB[jt], b_cst[k]], [B_P[2 * c]])
                    mm(PS[2 * c + 1][:, :], AB[:, jt, c * 256 + 128:c * 256 + 256], cst[:, k, 1, :], jt == 0, jt == 15,
                       [b_AB[jt], b_cst[k]], [B_P[2 * c + 1]])
            for c in range(4):
                q_, bq = Qs[c % 2], b_Qs[c % 2]
                act(q_[:, :], PS[2 * c + 1][:, :], AF.Copy, [B_P[2 * c + 1]], bq)
                dve_tt(YT[:, c, :], PS[2 * c][:, :], q_[:, :], ALU.add, [B_P[2 * c]] + bq, [b_YT])
                dve_tt(YTm[:, c, :], PS[2 * c][:, ::-1], q_[:, ::-1], ALU.subtract, [B_P[2 * c]] + bq, [b_YTm])
            if u == 0:
                f3(YT, b_YT, [(1, 512, 0)], 0)
                f3(YTm, b_YTm, [(1536, 512, 0)], 0)
            else:
                for c in range(4):
                    dve_copy(YT[:, c, 511:512], y0[:, c:c + 1], [b_y0], [b_YT])
                f3(YT, b_YT, [(513, 511, 0), (0, 1, 511)], 0)
                f3(YTm, b_YTm, [(1024, 512, 0)], 0)
        if not last:
            t0, n = BLOCKS[4]
            for ji, jt in enumerate([16, 17]):
                k = cnt["cst"] % NCST
                cnt["cst"] += 1
                dma(cst[:, k, :, 0:CTX], csc_d[:, ji, :, :], SL["cst%d" % k], writes=[b_cst[k]])
                for c in range(4):
                    mm(PS[2 + c][:, :n], AB[:, jt, c * 256:c * 256 + 128], cst[:, k, 0, :n], ji == 0, False,
                       [b_AB[jt], b_cst[k]], [B_P[2 + c]])
                    mm(PS[2 + c][:, :n], AB[:, jt, c * 256 + 128:c * 256 + 256], cst[:, k, 1, :n], False, ji == 1,
                       [b_AB[jt], b_cst[k]], [B_P[2 + c]])
            for c in range(4):
                if c % 2 == 0:
                    act(YT[:, c, :n], PS[2 + c][:, :n], AF.Copy, [B_P[2 + c]], [b_YT])
                else:
                    dve_copy(YT[:, c, :n], PS[2 + c][:, :n], [B_P[2 + c]], [b_YT])
            f3(YT, b_YT, [(t0, n, 0)], 1)
        while ada_todo:
            ada_piece(ada_next, ada_todo.pop(0), b_wa, bank=2, stage=next_stage())
        if ada_next is not None:
            ada_finish(ada_next)

    def out_proj(l, half, b, mix, b_mix, sidx, banks, dcs=range(8)):
        t0, n = BLOCKS[b]
        toks = []
        for dc in dcs:
            k = cnt["wos"] % NWOS
            cnt["wos"] += 1
            dma(wos[:, k], wo_d[l, half, dc], SL["wos%d" % k], writes=[B_wos[k]], eng="pool")
            pb = banks[dc % len(banks)]
            ps, bps = PS[pb], [B_P[pb]]
            for k4 in range(4):
                mm(ps[:, :n], wos[:, k, k4, :], mix[:, k4, :n], k4 == 0, k4 == 3, [B_wos[k], b_mix], bps)
            g_ap = modT[:, l * 48 + (16 + dc) * 2 + sidx:l * 48 + (16 + dc) * 2 + sidx + 1]
            toks.append(dve_stt(XT[:, dc, t0:t0 + n], ps[:, :n], g_ap, XT[:, dc, t0:t0 + n], ALU.mult, ALU.add,
                                bps + [B_XT[b], B_mod[l]], [B_XT[b]]))
        return toks

    def qk_norm(ps, b_ps, n, gvec, out_ap, b_out, b_sq, b_tmpA, b_rstd, sqslot):
        act(sq[:, sqslot, :n], ps[:, :n], AF.Square, [b_ps], [b_sq])
        mm(PS[7][:, :n], bd[:], sq[:, sqslot, :n], True, True, [b_sq, B_misc], [B_P[7]])
        rstd_from_ss(PS[7], n, 1.0 / 64, B_P[7], b_tmpA, b_rstd)
        dve_stt(out_ap, ps[:, :n], gvec, rstd[:, :n], ALU.mult, ALU.mult, [b_ps, b_rstd, B_const, B_qg8], [b_out])

    def phase_attn(l, last, nxt):
        names = ["sq0", "sq1", "tmpA", "rstd", "wv0", "wv1", "wv2", "wv3"]
        bs = new_phase_bufs(st_scratch, names)
        b_sq, b_tmpA, b_rstd, b_wv = bs[0:2], bs[2], bs[3], bs[4:8]
        r1b = new_phase_bufs(r1_state, ["kT%d" % b for b in range(5)] + ["Vp%d" % j for j in range(18)])
        b_kT, b_Vp = r1b[0:5], r1b[5:]
        k_slots = [load_w(12 + p_, l) for p_ in range(4)]
        for fcv in range(4):
            dma(wv[:, :, fcv * 128:(fcv + 1) * 128], win_d[l, 16 + fcv], SL["wv%d" % fcv], writes=[b_wv[fcv]], eng="pool")
        for jt in range(18):
            s.op("dve", (lambda jt_: (lambda e: e.memset(Vp[:, jt_, :, 64:128], 1.0)))(jt), writes=[b_Vp[jt]])
        i = 0
        vt = [0]

        def v_tile():
            jt = vt[0]
            if jt >= 18:
                return
            vt[0] += 1
            bb = min(jt // 4, 4)
            pb = 4 + jt % 2
            for kc in range(8):
                mm(PS[pb][:, :], xnT[:, kc, jt * 128:(jt + 1) * 128], wv[:, kc, :], kc == 0, kc == 7,
                   [B_xn[bb]] + b_wv, [B_P[pb]])
            src = PS[pb][:, :].rearrange("p (a t d) -> p a t d", t=2, d=64)
            act(Vp[:, jt, :, 0:64], src[:, :, 0, :], AF.Copy, [B_P[pb]], [b_Vp[jt]])
            dve_copy(Vp[:, jt, :, 128:192], src[:, :, 1, :], [B_P[pb]], [b_Vp[jt]])

        def k_chain(p, b, pp, ii):
            t0, n = BLOCKS[b]
            sb_ = 6 + ii % 2
            mm(PS[sb_][:, :n], bd[:], sq[:, ii % 2, :n], True, True, [b_sq[ii % 2], B_misc], [B_P[sb_]])
            rstd_from_ss(PS[sb_], n, 1.0 / 64, B_P[sb_], b_tmpA, b_rstd)
            dve_stt(kT[:, p, t0:t0 + n], PS[pp][:, :n], kg[:, l:l + 1], rstd[:, :n], ALU.mult, ALU.mult,
                    [B_P[pp], b_rstd, B_const], [b_kT[b]])

        pend = None
        for p in range(4):
            k = k_slots[p]
            for b, (t0, n) in enumerate(BLOCKS):
                pp = i % 4
                proj(k, b, PS[pp], B_P[pp])
                act(sq[:, i % 2, :n], PS[pp][:, :n], AF.Square, [B_P[pp]], [b_sq[i % 2]])
                if pend is not None:
                    k_chain(*pend)
                pend = (p, b, pp, i)
                i += 1
                if i >= 3:
                    v_tile()
        k_chain(*pend)
        while vt[0] < 18:
            v_tile()
        if stop_after == "a1":
            return
        names = ["sq0", "sq1", "tmpA", "rstd", "qz", "ngT", "mixA", "PT0a", "PT0b", "PT1a", "PT1b", "tmpB2"]
        bs = new_phase_bufs(st_scratch, names)
        b_sq, b_tmpA, b_rstd, b_qz, b_ngT, b_mixA = bs[0:2], bs[2], bs[3], bs[4], bs[5], bs[6]
        b_PT = (bs[7:9], bs[9:11])
        b_tmpB2 = bs[11]
        nrm = {"i": 0}
        b_lnd, b_wgt = b_tmpA, b_rstd

        b_rstdN = Buf("rstdN")
        b_rstdN.r = list(b_tmpB2.r)

        def norm_head(bb):
            t0_, n_ = BLOCKS[bb]
            for kc in range(8):
                k_ = nrm["i"] % 2
                nrm["i"] += 1
                dve_tt(sq[:, k_, :n_], XT[:, kc, t0_:t0_ + n_], XT[:, kc, t0_:t0_ + n_], ALU.mult, [B_XT[bb]], [b_sq[k_]])
                mm(PS[7][:, :n_], ones[:], sq[:, k_, :n_], kc == 0, kc == 7, [b_sq[k_], B_misc], [B_P[7]])
            act(rstdN[:, :n_], PS[7][:, :n_], AF.Ln, [B_P[7], B_misc], [b_rstdN], bias=epsT[:, 0:1], scale=1.0 / DM)
            act(rstdN[:, :n_], rstdN[:, :n_], AF.Exp, [b_rstdN], [b_rstdN], scale=-0.5)

        def norm_tail(bb, kc, use_act=False):
            tmp_, b_tmp_ = (tmpB2, b_tmpB2) if (not use_act or kc % 2 == 0) else (tmpA, b_tmpA)
            t0_, n_ = BLOCKS[bb]
            sx = 0 if bb < 4 else 1
            g_ap = gsT[:, nxt * 16 + kc * 2 + sx:nxt * 16 + kc * 2 + sx + 1]
            sh_ap = modT[:, nxt * 48 + kc * 2 + sx:nxt * 48 + kc * 2 + sx + 1]
            dve_stt(tmp_[:, :n_], XT[:, kc, t0_:t0_ + n_], g_ap, rstdN[:, :n_], ALU.mult, ALU.mult,
                    [B_XT[bb], b_rstdN, B_mod[nxt]], [b_tmp_])
            if use_act:
                act(xnT[:, kc, t0_:t0_ + n_], tmp_[:, :n_], AF.Identity, [b_tmp_, B_mod[nxt]], [B_xn[bb]], bias=sh_ap)
            else:
                dve_ts(xnT[:, kc, t0_:t0_ + n_], tmp_[:, :n_], sh_ap, ALU.add, [b_tmp_, B_mod[nxt]], [B_xn[bb]])

        s.op("dve", lambda e: e.memset(qz[64:128, :, :, 0, :], 0.0), writes=[b_qz])
        s.op("dve", lambda e: e.memset(qz[0:64, :, :, 1, :], 0.0), writes=[b_qz])
        nblk = 4 if last else 5
        SB = ((PS[2], PS[3], B_P[2], B_P[3]), (PS[4], PS[5], B_P[4], B_P[5]))
        ic = {"v": i}

        def q_part(bb):
            t0_, n_ = BLOCKS[bb]
            ntile_ = n_ // 128

            def q_chain(p, pp, ii):
                mm(PS[7][:, :n_], bd[:], sq[:, ii % 2, :n_], True, True, [b_sq[ii % 2], B_misc], [B_P[7]])
                rstd_from_ss(PS[7], n_, 1.0 / 64, B_P[7], b_tmpA, b_rstd)
                for hh in range(2):
                    pr = slice(hh * 64, hh * 64 + 64)
                    dve_stt(qz[pr, p, 0:ntile_, hh, :], PS[pp][pr, :n_].rearrange("p (t q) -> p t q", q=128),
                            qg8[pr, l:l + 1], rstd[pr, :n_].rearrange("p (t q) -> p t q", q=128), ALU.mult, ALU.mult,
                            [B_P[pp], b_rstd, B_qg8], [b_qz])

            pend = None
            for p in range(4):
                pp = 2 + p
                k = load_w(8 + p, l)
                proj(k, bb, PS[pp], B_P[pp])
                act(sq[:, ic["v"] % 2, :n_], PS[pp][:, :n_], AF.Square, [B_P[pp]], [b_sq[ic["v"] % 2]])
                if pend is not None:
                    q_chain(*pend)
                pend = (p, pp, ic["v"])
                ic["v"] += 1
            q_chain(*pend)

        def ng_part(bb, ps_=range(4)):
            t0_, n_ = BLOCKS[bb]
            for p in ps_:
                pp = 2 + p
                k = load_w(20 + p, l)
                proj(k, bb, PS[pp], B_P[pp])
                act(ngT[:, p, :n_], PS[pp][:, :n_], AF.Silu, [B_P[pp]], [b_ngT])

        q_part(0)
        ng_part(0)
        for b in range(nblk):
            t0, n = BLOCKS[b]
            ntile = n // 128
            sidx = 0 if b < 4 else 1
            tail_todo = []
            if nxt is not None and b > 0:
                norm_head(b - 1)
                tail_todo = [(b - 1, kc) for kc in range(8)]
            def chunks_of(t):
                if b < 4:
                    return [("l", c) for c in tile_chunks(t)] + [("c", 16), ("c", 17)]
                return [("c", 16), ("c", 17)]

            steps = []
            kbs = {}
            for p in range(4):
                for tl in range(ntile):
                    t = t0 // 128 + tl
                    ch = chunks_of(t)
                    groups = [ch[0:4], ch[4:]] if len(ch) > 4 else [ch]
                    for gi, grp_ in enumerate(groups):
                        steps.append((p, t, tl, grp_, gi == 0, gi == len(groups) - 1))

            def emit_qk(si):
                p, t, tl, grp_, first, lastg = steps[si]
                if b < 4 and p not in kbs:
                    kb = cnt["bmt"] % 2
                    cnt["bmt"] += 1
                    dma(bmt[:, kb], bm_d[l, p], SL["bmt%d" % kb], writes=[B_bmt[kb]], eng="pool")
                    kbs[p] = kb
                par = si % 2
                banks = SB[par]
                for ci, (kind, c) in enumerate(grp_):
                    ps, bps = banks[ci // 2], banks[2 + ci // 2]
                    cc = (ci % 2) * 256
                    mm(ps[:, cc:cc + 256], kT[:, p, c * 128:(c + 1) * 128], qz[:, p, tl, :, :], True,
                       kind == "c", [b_kT[min(c // 4, 4)], b_qz], [bps])
                    if kind == "l":
                        v = CHUNK_VAR[(t, c)]
                        mm(ps[:, cc:cc + 256], ident[:], bmt[:, kbs[p], v, :], False, True,
                           [B_const, B_bmt[kbs[p]]], [bps])
                ncol = len(grp_) * 256
                base = 1024 + par * 1024
                act(PT[:, par, 0:ncol], psall[:, base:base + ncol], AF.Exp, [banks[2], banks[3]],
                    [b_PT[par][0], b_PT[par][1]])

            def emit_pv(si):
                p, t, tl, grp_, first, lastg = steps[si]
                par = si % 2
                ob = (6, 7) if p % 2 == 0 else (0, 1)
                for hh in range(2):
                    pso, bo = PS[ob[hh]], B_P[ob[hh]]
                    for ci, (kind, c) in enumerate(grp_):
                        mm(pso[:, tl * 128:(tl + 1) * 128], Vp[:, c, p, hh * 64:hh * 64 + 128],
                           PT[:, par, ci * 256 + hh * 128:ci * 256 + hh * 128 + 128],
                           first and ci == 0, lastg and ci == len(grp_) - 1, [b_Vp[c], b_PT[par][ci // 2]], [bo])

            def emit_norm(p):
                ob = (6, 7) if p % 2 == 0 else (0, 1)
                pA, pB, bA, bB = PS[ob[0]], PS[ob[1]], B_P[ob[0]], B_P[ob[1]]
                act(lnd[64:128, :n], pA[64:128, :n], AF.Ln, [bA], [b_lnd])
                act(lnd[0:64, :n], pB[0:64, :n], AF.Ln, [bB], [b_lnd])
                act(wgt[0:64, :n], lnd[64:128, :n], AF.Exp, [b_lnd], [b_wgt], scale=-1.0)
                act(wgt[64:128, :n], lnd[0:64, :n], AF.Exp, [b_lnd], [b_wgt], scale=-1.0)
                dve_tt(wgt[:, :n], wgt[:, :n], ngT[:, p, :n], ALU.mult, [b_wgt, b_ngT], [b_wgt])
                dve_tt(mixA[0:64, p, :n], pA[0:64, :n], wgt[0:64, :n], ALU.mult, [bA, b_wgt], [b_mixA])
                dve_tt(mixA[64:128, p, :n], pB[64:128, :n], wgt[64:128, :n], ALU.mult, [bB, b_wgt], [b_mixA])

            emit_qk(0)
            for si in range(len(steps)):
                if si + 1 < len(steps):
                    emit_qk(si + 1)
                emit_pv(si)
                if si + 1 == len(steps) or steps[si + 1][0] != steps[si][0]:
                    emit_norm(steps[si][0])
                if tail_todo and si >= 2 and si % 2 == 0:
                    norm_tail(*tail_todo.pop(0))
            while tail_todo:
                norm_tail(*tail_todo.pop(0))
            if stop_after == "a2_n":
                return
            if b + 1 < nblk:
                q_part(b + 1)
            otoks = out_proj(l, 1, b, mixA, b_mixA, sidx, (6, 7, 0, 1))
            if last and l == layers[-1] and b < 3:
                dma(oT_d[:, :, t0:t0 + n], XT[:, :, t0:t0 + n], SL["out"], reads=[B_XT[b]])
                stored.add(b)
            elif last and l == layers[-1] and b == 3:
                for dc in range(8):
                    s.dma((lambda o_, i_: (lambda e: e.dma_start(out=o_, in_=i_)))(oT_d[:, dc, t0:t0 + n], XT[:, dc, t0:t0 + n]),
                          SL["out"], extra=[otoks[dc]])
                stored.add(b)
            if b + 1 < nblk:
                ng_part(b + 1)
        if nxt is not None:
            norm_head(nblk - 1)
            for kc in range(8):
                norm_tail(nblk - 1, kc, use_act=True)

    for l in layers:
        last = (l == final_layer)
        if stop_after == "init":
            break
        if l == layers[0]:
            phase_norm(l, with_ada=True)
        if stop_after == "norm":
            break
        nxt = layers[layers.index(l) + 1] if layers.index(l) + 1 < len(layers) else None
        phase_fourier(l, last, nxt)
        if stop_after == "fourier":
            break
        phase_attn(l, last, nxt)

    dbg_map = {"modT": (modT, [128, nL * 48], F32), "gsT": (gsT, [128, nL * 16], F32),
               "xnT": (xnT, [128, 8, NT], BF16), "AB": (AB, [128, 18, 1024], BF16),
               "kT": (kT, [128, 4, NT], BF16), "Vp": (Vp, [128, 18, 4, 192], BF16),
               "mixF": (mixF, [128, 4, 512], BF16), "mixA": (mixA, [128, 4, 512], BF16),
               "YT": (YT, [128, 4, 512], BF16), "PT": (PT, [128, 2, 1024], BF16)}
    all_bufs = ([B_const, B_misc, B_cTs, B_qg8] + B_XT + B_xn + B_P + B_wst + B_wos + B_bmt + B_mod
                + st_scratch["bufs"] + r1_state["bufs"])
    for name in dbg:
        t_, shp, dt_ = dbg_map[name]
        d_ = nc.dram_tensor("dbg_" + name, shp, dt_, kind="ExternalOutput").ap()
        if len(shp) == 2:
            dma(d_[:, :], t_[:], SL["out"], reads=all_bufs)
        elif len(shp) == 3:
            dma(d_[:, :, :], t_[:], SL["out"], reads=all_bufs)
        else:
            dma(d_[:, :, :, :], t_[:], SL["out"], reads=all_bufs)

    outs = []
    for b in range(4):
        if b in stored:
            continue
        t0, n = BLOCKS[b]
        outs.append(s.dma((lambda t0_, n_: (lambda e: e.dma_start(out=oT_d[:, :, t0_:t0_ + n_], in_=XT[:, :, t0_:t0_ + n_])))(t0, n),
                          SL["out"], reads=[B_XT[b]]))
    s.finalize(final_waits=[("d", SL["out"], s.dma_count[SL["out"]])])
    st.close()
    return nc


def _dft_consts():
    j = np.arange(SEQ, dtype=np.float64)
    ang = 2.0 * np.pi * ((j[:, None] * j[None, :]) % SEQ) / SEQ
    sc = (SEQ * 128) ** -0.5
    C = np.cos(ang) * sc
    S = -np.sin(ang) * sc
    cs = np.stack([C.reshape(16, 128, SEQ), S.reshape(16, 128, SEQ)], axis=2)
    jc = np.arange(CTX, dtype=np.float64)
    angc = 2.0 * np.pi * ((jc[:, None] * jc[None, :]) % CTX) / CTX
    scc = (CTX * 128) ** -0.5
    Cc = (np.cos(angc) * scc).reshape(2, 128, CTX)
    Sc = (-np.sin(angc) * scc).reshape(2, 128, CTX)
    csc = np.stack([Cc, Sc], axis=2).transpose(1, 0, 2, 3)
    m = np.arange(128, dtype=np.float64)
    a128 = 2.0 * np.pi * ((m[:, None] * m[None, :]) % 128) / 128
    cs128 = np.concatenate([np.cos(a128), np.sin(a128)], axis=1)
    bf = ml_dtypes.bfloat16
    return (np.ascontiguousarray(cs).astype(bf), np.ascontiguousarray(csc).astype(bf),
            cs128.astype(bf), np.eye(128).astype(bf))


def _bias_tables(rel_bias):
    nl = rel_bias.shape[0]
    kc = np.arange(64)
    qc = np.arange(64)
    cs0 = np.clip(qc - 8, 0, 48)
    in_win = (kc[:, None] >= cs0[None, :]) & (kc[:, None] < cs0[None, :] + 16)
    dc_idx = np.clip(kc[:, None] - qc[None, :], -15, 15) + 15
    out = np.full((nl, 8, NV, 128, 128), NEG, dtype=np.float32)
    for (D, allowed), v in VARIANTS.items():
        for kp in range(2):
            for qp in range(2):
                if not allowed[kp][qp]:
                    continue
                dr = 2 * D + kp - qp
                blk = rel_bias[:, :, dr + 7, :][:, :, dc_idx]
                blk = np.where(in_win[None, None], blk, np.float32(NEG))
                out[:, :, v, kp * 64:(kp + 1) * 64, qp * 64:(qp + 1) * 64] = blk
    out = out.reshape(nl, 4, 2, NV, 128, 128).transpose(0, 1, 4, 3, 2, 5)
    return np.ascontiguousarray(out).reshape(nl, 4, 128, NV, 256)


def _prep_shared(inputs):
    f = np.float32
    w_ada = np.asarray(inputs["w_ada"], f)
    w_in = np.asarray(inputs["w_in"], f)
    w_four = np.asarray(inputs["w_four"], f)
    w_out = np.asarray(inputs["w_out"], f)
    nl = w_in.shape[0]
    cs, csc, cs128, ident = _dft_consts()
    sh = {
        "wada": np.ascontiguousarray(w_ada.reshape(nl, 8, 128, 24, 128).transpose(0, 3, 2, 1, 4)),
        "win": np.ascontiguousarray(w_in.reshape(nl, 8, 128, 24, 128).transpose(0, 3, 2, 1, 4)),
        "w4": np.ascontiguousarray(w_four.reshape(nl, 4, 128, 512).transpose(0, 2, 1, 3)),
        "wo": np.ascontiguousarray(w_out.reshape(nl, 2, 4, 128, 8, 128).transpose(0, 1, 4, 3, 2, 5)),
        "bm": _bias_tables(np.asarray(inputs["rel_bias"], f)),
        "cs": cs, "csc": csc, "cs128": cs128, "ident": ident,
    }
    b_ada = np.asarray(inputs["b_ada"], f)
    norm_g = np.asarray(inputs["norm_g"], f)
    bada = b_ada.reshape(nl, 24, 128).transpose(2, 0, 1)
    sh["bada"] = np.ascontiguousarray(np.repeat(bada[..., None], 2, axis=-1)).reshape(128, nl * 48)
    ng = norm_g.reshape(nl, 8, 128).transpose(2, 0, 1)
    sh["ng2"] = np.ascontiguousarray(np.repeat(ng[..., None], 2, axis=-1)).reshape(128, nl * 16)
    sh["qg"] = np.ascontiguousarray(np.tile(np.asarray(inputs["q_norm_g"], f), (1, 2)).T)
    sh["kg"] = np.ascontiguousarray(np.tile(np.asarray(inputs["k_norm_g"], f), (1, 2)).T)
    return sh


def _prep_core(x_b, ctx_b, c_b, c_ctx):
    f = np.float32
    xt = np.concatenate([np.asarray(x_b, f).T, np.asarray(ctx_b, f).T], axis=1)
    xT = np.ascontiguousarray(xt.reshape(8, 128, NT).transpose(1, 0, 2))
    cc = np.stack([np.asarray(c_b, f), np.asarray(c_ctx, f)], axis=-1)
    cT = np.ascontiguousarray(cc.reshape(8, 128, 2).transpose(1, 0, 2))
    return {"xT": xT, "cT": cT}


_NC_CACHE = {}


def kernel(x, c, ctx, c_ctx, norm_g, w_ada, b_ada, w_in, w_four, q_norm_g, k_norm_g, rel_bias, w_out):
    inputs = dict(x=x, c=c, ctx=ctx, c_ctx=c_ctx, norm_g=norm_g, w_ada=w_ada, b_ada=b_ada, w_in=w_in,
                  w_four=w_four, q_norm_g=q_norm_g, k_norm_g=k_norm_g, rel_bias=rel_bias, w_out=w_out)
    shared = _prep_shared(inputs)
    nb = np.asarray(x).shape[0]
    in_maps = []
    for b in range(nb):
        m = dict(shared)
        m.update(_prep_core(x[b], ctx[b], c[b], c_ctx))
        in_maps.append(m)
    if "nc" not in _NC_CACHE:
        _NC_CACHE["nc"] = build_nc()
    nc = _NC_CACHE["nc"]
    res = run_bass_kernel_spmd(nc, in_maps, core_ids=list(range(nb)))
    out = np.empty((nb, SEQ, DM), dtype=np.float32)
    for b in range(nb):
        oT = np.asarray(res.results[b]["oT"], dtype=np.float32)
        out[b] = oT.transpose(1, 0, 2).reshape(DM, SEQ).T
    return out
```

```python
import contextlib
import numpy as np
import ml_dtypes
import concourse.bass as bass
import concourse.mybir as mybir
from concourse.bass_utils import run_bass_kernel_spmd

F32, BF16 = mybir.dt.float32, mybir.dt.bfloat16
AF = mybir.ActivationFunctionType
ALU = mybir.AluOpType

DEPTH = 4
DM = 1024
SEQ = 2048
CTX = 256
NT = SEQ + CTX
BLOCKS = [(0, 512), (512, 512), (1024, 512), (1536, 512), (2048, 256)]
EPS = 1e-6
NEG = -30000.0
ENGS = ("pe", "act", "dve", "pool", "sp")

RS = [min(max(r - 4, 0), 24) for r in range(32)]


def tile_chunks(t):
    lo = RS[2 * t] // 2
    hi = (RS[2 * t + 1] + 7) // 2
    return list(range(lo, hi + 1))


def chunk_allowed(t, c):
    return tuple(
        tuple(RS[2 * t + qp] <= 2 * c + kp <= RS[2 * t + qp] + 7 for qp in (0, 1)) for kp in (0, 1)
    )


def build_variants():
    var = {}
    cmap = {}
    for t in range(16):
        for c in tile_chunks(t):
            key = (c - t, chunk_allowed(t, c))
            if key not in var:
                var[key] = len(var)
            cmap[(t, c)] = var[key]
    return var, cmap


VARIANTS, CHUNK_VAR = build_variants()
NV = len(VARIANTS)


class Buf:
    __slots__ = ("name", "w", "r")

    def __init__(self, name=""):
        self.name = name
        self.w = None
        self.r = []


class Sched:
    def __init__(self, nc, n_dma):
        self.nc = nc
        self.q = {e: [] for e in ENGS}
        self.n_dma = n_dma
        self.dma_count = [0] * n_dma

    def _deps(self, reads, writes, extra):
        waits = list(extra)
        for b in reads:
            if b.w is not None:
                waits.append(b.w)
        for b in writes:
            if b.w is not None:
                waits.append(b.w)
            waits.extend(b.r)
        return waits

    def _commit(self, tok, reads, writes):
        for b in reads:
            b.r.append(tok)
        for b in writes:
            b.w = tok
            b.r = []

    def op(self, eng, fn, reads=(), writes=(), extra=()):
        waits = self._deps(reads, writes, extra)
        tok = ("e", eng, len(self.q[eng]))
        self.q[eng].append([fn, waits, False, None])
        self._commit(tok, reads, writes)
        return tok

    def dma(self, fn, slot, reads=(), writes=(), eng="sp", extra=()):
        waits = self._deps(reads, writes, extra)
        self.dma_count[slot] += 16
        tok = ("d", slot, self.dma_count[slot])
        self.q[eng].append([fn, waits, False, slot])
        self._commit(tok, reads, writes)
        return tok

    def finalize(self, final_waits=()):
        nc = self.nc
        for e in ENGS:
            for item in self.q[e]:
                for t in item[1]:
                    if t[0] == "e":
                        self.q[t[1]][t[2]][2] = True
        for t in final_waits:
            if t[0] == "e":
                self.q[t[1]][t[2]][2] = True
        cum = {}
        for e in ENGS:
            c = 0
            arr = []
            for item in self.q[e]:
                if item[2]:
                    c += 1
                arr.append(c)
            cum[e] = arr
        with contextlib.ExitStack() as st:
            esem = {e: st.enter_context(nc.semaphore("s_" + e)) for e in ENGS}
            dsem = [st.enter_context(nc.semaphore("d_%d" % i)) for i in range(self.n_dma)]
            block = st.enter_context(nc.Block())

            def resolve(t):
                if t[0] == "e":
                    return ("e", t[1]), esem[t[1]], cum[t[1]][t[2]]
                return ("d", t[1]), dsem[t[1]], t[2]

            def run(e, engobj, tail=()):
                waited = {}

                def do_waits(waits):
                    need = {}
                    for t in waits:
                        key, sem, val = resolve(t)
                        if key == ("e", "pe") and e == "pe":
                            continue
                        if waited.get(key, 0) >= val:
                            continue
                        if need.get(key, (None, 0))[1] < val:
                            need[key] = (sem, val)
                    for key, (sem, val) in need.items():
                        engobj.wait_ge(sem, val)
                        waited[key] = val

                for fn, waits, sig, dslot in self.q[e]:
                    do_waits(waits)
                    ins = fn(engobj)
                    if dslot is not None:
                        ins.then_inc(dsem[dslot], 16)
                    elif sig:
                        ins.then_inc(esem[e], 1)
                do_waits(tail)

            @block.tensor
            def _(eng):
                run("pe", eng)

            @block.scalar
            def _(eng):
                run("act", eng)

            @block.vector
            def _(eng):
                run("dve", eng)

            @block.gpsimd
            def _(eng):
                run("pool", eng)

            @block.sync
            def _(eng):
                run("sp", eng, tail=final_waits)


def build_nc(layers=tuple(range(DEPTH)), final_layer=DEPTH - 1, stop_after=None, dbg=()):
    nc = bass.Bass("TRN2", target_bir_lowering=False)
    nL = DEPTH

    def din(name, shape, dt=F32):
        return nc.dram_tensor(name, list(shape), dt, kind="ExternalInput").ap()

    xT_d = din("xT", [128, 8, NT])
    cT_d = din("cT", [128, 8, 2])
    bada_d = din("bada", [128, nL * 48])
    ng2_d = din("ng2", [128, nL * 16])
    qg_d = din("qg", [128, nL])
    kg_d = din("kg", [128, nL])
    wada_d = din("wada", [nL, 24, 128, 8, 128])
    win_d = din("win", [nL, 24, 128, 8, 128])
    w4_d = din("w4", [nL, 128, 4, 512])
    wo_d = din("wo", [nL, 2, 8, 128, 4, 128])
    bm_d = din("bm", [nL, 4, 128, NV, 256])
    cs_d = din("cs", [16, 128, 2, SEQ], BF16)
    csc_d = din("csc", [128, 2, 2, CTX], BF16)
    cs128_d = din("cs128", [128, 256], BF16)
    ident_d = din("ident", [128, 128], BF16)
    oT_d = nc.dram_tensor("oT", [128, 8, SEQ], F32, kind="ExternalOutput").ap()

    off = [16512]

    def alloc(name, shape, dt, at=None):
        nbytes = int(np.prod(shape[1:])) * (4 if dt == F32 else 2)
        nbytes = (nbytes + 31) // 32 * 32
        if at is None:
            o = off[0]
            off[0] += nbytes
        else:
            o = at
        return nc.alloc_sbuf_tensor_at(name, list(shape), dt, offset=o), o + nbytes

    NWST = 4
    NWOS = 4
    NCST = 3
    XT, _ = alloc("XT", [128, 8, NT], F32)
    xnT, _ = alloc("xnT", [128, 8, NT], BF16)
    r1 = off[0]
    AB, _ = alloc("AB", [128, 18, 1024], BF16, at=r1)
    kT, e1 = alloc("kT", [128, 4, NT], BF16, at=r1)
    Vp, e2 = alloc("Vp", [128, 18, 4, 192], BF16, at=e1)
    off[0] = e2
    wst, _ = alloc("wst", [128, NWST, 8, 128], BF16)
    wos, _ = alloc("wos", [128, NWOS, 4, 128], BF16)
    bmt, _ = alloc("bmt", [128, 2, NV, 256], BF16)
    ident, _ = alloc("ident", [128, 128], BF16)
    ones, _ = alloc("ones", [128, 128], BF16)
    bd, _ = alloc("bd", [128, 128], BF16)
    cs128, _ = alloc("cs128", [128, 256], BF16)
    cT, _ = alloc("cT", [128, 8, 2], F32)
    cTs, _ = alloc("cTs", [128, 8, 2], BF16)
    bada, _ = alloc("bada", [128, nL * 48], F32)
    ng2, _ = alloc("ng2", [128, nL * 16], F32)
    qg, _ = alloc("qg", [128, nL], F32)
    kg, _ = alloc("kg", [128, nL], F32)
    qg8, _ = alloc("qg8", [128, nL], F32)
    modT, _ = alloc("modT", [128, nL * 48], F32)
    gsT, _ = alloc("gsT", [128, nL * 16], F32)
    epsT, _ = alloc("epsT", [128, 1], F32)
    rstdN, _ = alloc("rstdN", [128, 512], F32)
    y0, _ = alloc("y0", [128, 16], BF16)
    c0v, _ = alloc("c0v", [128, 16], BF16)
    S0 = off[0]
    sq, e = alloc("sq", [128, 4, 512], BF16, at=S0)
    tmpA, e = alloc("tmpA", [128, 512], F32, at=e)
    rstd, e = alloc("rstd", [128, 512], F32, at=e)
    S1 = e
    tmpB, eN = alloc("tmpB", [128, 2, 512], F32, at=e)
    cst, e = alloc("cst", [128, NCST, 2, 512], BF16, at=S0)
    wac = [alloc("wac%d" % k_, [128, 8, 128], BF16, at=S0 + k_ * 2048)[0] for k_ in range(NCST)]
    wa0s = [alloc("wa0s%d" % k_, [128, 8, 128], BF16, at=r1 + k_ * 2048)[0] for k_ in range(8)]
    YT, e = alloc("YT", [128, 4, 512], BF16, at=e)
    YTm, e = alloc("YTm", [128, 4, 512], BF16, at=e)
    mixF, e = alloc("mixF", [128, 4, 512], BF16, at=e)
    fg_off = e
    fg, e = alloc("fg", [128, 2, 512], BF16, at=e)
    fin_off = e
    finT, e = alloc("finT", [128, 2, 512], BF16, at=e)
    Qs0, _ = alloc("Qs0", [128, 512], F32, at=fg_off)
    Qs1, _ = alloc("Qs1", [128, 512], F32, at=fin_off)
    w4s, e = alloc("w4s", [128, 4, 512], BF16, at=e)
    wa_off = e
    wa, eF = alloc("wa", [128, 1, 8, 128], BF16, at=e)
    wosx, _ = alloc("wosx", [128, 2, 4, 128], BF16, at=wa_off)
    wosy, _ = alloc("wosy", [128, 2, 4, 128], BF16, at=fin_off)
    wv, eA1 = alloc("wv", [128, 8, 512], BF16, at=S1)
    qz, e = alloc("qz", [128, 4, 4, 2, 128], BF16, at=S1)
    ngT, e = alloc("ngT", [128, 4, 512], BF16, at=e)
    mixA, e = alloc("mixA", [128, 4, 512], BF16, at=e)
    PT, eA2 = alloc("PT", [128, 2, 1024], BF16, at=e)
    lnd, wgt = tmpA, rstd
    tmpB2, _ = alloc("tmpB2", [128, 512], F32, at=S0 + 2048)
    top = max(eN, eF, eA1, eA2)
    assert top <= 229344, top

    st = contextlib.ExitStack()
    psall = st.enter_context(nc.psum_tensor("psall", [128, 4096], F32))
    PS = [psall[:, i * 512:(i + 1) * 512] for i in range(8)]

    slot_names = (["a%d" % i for i in range(8)] + ["const", "x0", "x1", "x2", "x3", "x4", "wa0", "wa1", "bmt0", "bmt1", "w4", "wv0", "wv1", "wv2", "wv3", "out"]
                  + ["wst%d" % i for i in range(NWST)] + ["wos%d" % i for i in range(NWOS)]
                  + ["cst%d" % i for i in range(NCST)])
    SL = {n: i for i, n in enumerate(slot_names)}
    s = Sched(nc, len(slot_names))

    B_XT = [Buf("XT%d" % b) for b in range(5)]
    B_xn = [Buf("xn%d" % b) for b in range(5)]
    B_P = [Buf("P%d" % i) for i in range(8)]
    B_const = Buf("const")
    B_wst = [Buf("wst%d" % i) for i in range(NWST)]
    B_wos = [Buf("wos%d" % i) for i in range(NWOS)]
    B_bmt = [Buf("bmt0"), Buf("bmt1")]
    B_mod = [Buf("mod%d" % l) for l in range(nL)]
    B_cTs = Buf("cTs")
    B_qg8 = Buf("qg8")
    st_scratch = {"bufs": []}
    r1_state = {"bufs": []}

    def new_phase_bufs(state, names):
        toks = []
        for b in state["bufs"]:
            if b.w is not None:
                toks.append(b.w)
            toks.extend(b.r)
        out = []
        for n in names:
            b = Buf(n)
            b.r = list(toks)
            out.append(b)
        state["bufs"] = out
        return out

    cnt = {"wst": 0, "wos": 0, "bmt": 0, "cst": 0, "wa": 0, "pp": 0, "wosF": 0}

    def mm(out, lhsT, rhs, start, stop, reads, writes):
        return s.op("pe", lambda e: e.matmul(out, lhsT=lhsT, rhs=rhs, start=start, stop=stop), reads, writes)

    def act(out, in_, func, reads, writes, bias=None, scale=None):
        kw = {}
        if bias is not None:
            kw["bias"] = bias
        if scale is not None:
            kw["scale"] = scale
        return s.op("act", lambda e: e.activation(out=out, in_=in_, func=func, **kw), reads, writes)

    def dve_tt(out, in0, in1, op, reads, writes):
        return s.op("dve", lambda e: e.tensor_tensor(out=out, in0=in0, in1=in1, op=op), reads, writes)

    def dve_stt(out, in0, scalar, in1, op0, op1, reads, writes):
        return s.op("dve", lambda e: e.scalar_tensor_tensor(out=out, in0=in0, scalar=scalar, in1=in1,
                                                            op0=op0, op1=op1), reads, writes)

    def dve_ts(out, in0, scalar1, op0, reads, writes):
        return s.op("dve", lambda e: e.tensor_scalar(out=out, in0=in0, scalar1=scalar1, scalar2=None, op0=op0),
                    reads, writes)

    def dve_copy(out, in_, reads, writes):
        return s.op("dve", lambda e: e.tensor_copy(out=out, in_=in_), reads, writes)

    def dma(out, in_, slot, reads=(), writes=(), eng="sp"):
        return s.dma(lambda e: e.dma_start(out=out, in_=in_), slot, reads=reads, writes=writes, eng=eng)

    def load_w(fc, l):
        k = cnt["wst"] % NWST
        cnt["wst"] += 1
        dma(wst[:, k], win_d[l, fc], SL["wst%d" % k], writes=[B_wst[k]], eng="pool")
        return k

    cd = SL["const"]
    for dst, src in ((ident[:], ident_d[:, :]), (cs128[:], cs128_d[:, :]),
                     (cT[:], cT_d[:, :, :]), (bada[:], bada_d[:, :]), (ng2[:], ng2_d[:, :]),
                     (qg[:], qg_d[:, :]), (kg[:], kg_d[:, :])):
        dma(dst, src, cd)
    const_tok = ("d", cd, s.dma_count[cd])
    B_const.w = const_tok
    for b, (t0, n) in enumerate(BLOCKS):
        dma(XT[:, :, t0:t0 + n], xT_d[:, :, t0:t0 + n], SL["x%d" % b], writes=[B_XT[b]])
    B_misc = Buf("misc")
    s.op("pool", lambda e: e.memset(ones[:], 1.0), writes=[B_misc])
    s.op("pool", lambda e: e.memset(bd[:], 0.0), writes=[B_misc])
    s.op("pool", lambda e: e.memset(bd[0:64, 0:64], 1.0), writes=[B_misc])
    s.op("pool", lambda e: e.memset(bd[64:128, 64:128], 1.0), writes=[B_misc])
    s.op("pool", lambda e: e.memset(epsT[:], EPS), writes=[B_misc])
    s.op("pool", lambda e: e.memset(c0v[:], 1.0 / 512.0), writes=[B_misc])
    act(cTs[:], cT[:], AF.Silu, [B_const], [B_cTs])
    dve_ts(qg8[:], qg[:], 0.125, ALU.mult, [B_const], [B_qg8])

    ada_state = {}
    stored = set()

    def ada_piece(l, fcg, b_wa, bank=0, stage=None):
        if stage is None:
            wt, bw, sl = wa[:, 0], b_wa[0], SL["wa0"]
        else:
            wt, bw, sl = stage
        dma(wt, wada_d[l, fcg], sl, writes=[bw], eng="pool")
        for kc in range(8):
            mm(PS[bank][:, 0:2], wt[:, kc, :], cTs[:, kc, :], kc == 0, kc == 7, [bw, B_cTs], [B_P[bank]])
        c0 = l * 48 + fcg * 2
        dve_tt(modT[:, c0:c0 + 2], PS[bank][:, 0:2], bada[:, c0:c0 + 2], ALU.add, [B_P[bank], B_const], [B_mod[l]])

    def ada_finish(l):
        dve_ts(gsT[:, l * 16:(l + 1) * 16], modT[:, l * 48 + 16:l * 48 + 32], 1.0, ALU.add, [B_mod[l]], [B_mod[l]])
        dve_tt(gsT[:, l * 16:(l + 1) * 16], gsT[:, l * 16:(l + 1) * 16], ng2[:, l * 16:(l + 1) * 16], ALU.mult,
               [B_mod[l], B_const], [B_mod[l]])

    def phase_ada_standalone(l):
        b_wa = new_phase_bufs(st_scratch, ["wa0", "wa1"])
        for fcg in range(24):
            ada_piece(l, fcg, b_wa)
        ada_finish(l)

    def rstd_from_ss(ps, n, inv_n, b_ps, b_tmpA, b_rstd):
        act(tmpA[:, :n], ps[:, :n], AF.Ln, [b_ps, B_misc], [b_tmpA], bias=epsT[:, 0:1], scale=inv_n)
        act(rstd[:, :n], tmpA[:, :n], AF.Exp, [b_tmpA], [b_rstd], scale=-0.5)

    def phase_norm(l, with_ada=False):
        names = ["sq0", "sq1", "sq2", "sq3", "tmpA", "rstd", "tmpB0", "tmpB1", "wa0", "wa1"]
        bs = new_phase_bufs(st_scratch, names)
        b_sq, b_tmpA, b_rstd, b_tmpB, b_wa = bs[0:4], bs[4], bs[5], bs[6:8], bs[8:10]
        ada_todo = list(range(24)) if with_ada else []
        b_a0 = [Buf("a0_%d" % k_) for k_ in range(8)]
        na = [0]

        def stage0():
            k_ = na[0] % 8
            na[0] += 1
            return (wa0s[k_][:], b_a0[k_], SL["a%d" % k_])

        i = 0
        for b, (t0, n) in enumerate(BLOCKS):
            for kc in range(8):
                k = i % 4
                i += 1
                act(sq[:, k, :n], XT[:, kc, t0:t0 + n], AF.Square, [B_XT[b]], [b_sq[k]])
                mm(PS[1 + b][:, :n], ones[:], sq[:, k, :n], kc == 0, kc == 7, [b_sq[k], B_misc], [B_P[1 + b]])
            for _ in range(5):
                if ada_todo:
                    ada_piece(l, ada_todo.pop(0), b_wa, stage=stage0())
        while ada_todo:
            ada_piece(l, ada_todo.pop(0), b_wa, stage=stage0())
        r1_state["bufs"] = r1_state["bufs"] + b_a0
        if with_ada:
            ada_finish(l)
        for b, (t0, n) in enumerate(BLOCKS):
            sidx = 0 if b < 4 else 1
            rstd_from_ss(PS[1 + b], n, 1.0 / DM, B_P[1 + b], b_tmpA, b_rstd)
            for kc in range(8):
                k = kc % 2
                g_ap = gsT[:, l * 16 + kc * 2 + sidx:l * 16 + kc * 2 + sidx + 1]
                sh_ap = modT[:, l * 48 + kc * 2 + sidx:l * 48 + kc * 2 + sidx + 1]
                dve_stt(tmpB[:, k, :n], XT[:, kc, t0:t0 + n], g_ap, rstd[:, :n], ALU.mult, ALU.mult,
                        [B_XT[b], b_rstd, B_mod[l]], [b_tmpB[k]])
                act(xnT[:, kc, t0:t0 + n], tmpB[:, k, :n], AF.Identity, [b_tmpB[k], B_mod[l]], [B_xn[b]], bias=sh_ap)

    def proj(fc_slot, b, ps, b_ps):
        t0, n = BLOCKS[b]
        for kc in range(8):
            mm(ps[:, :n], wst[:, fc_slot, kc, :], xnT[:, kc, t0:t0 + n], kc == 0, kc == 7,
               [B_wst[fc_slot], B_xn[b]], [b_ps])

    def next_pp():
        k = cnt["pp"] % 2
        cnt["pp"] += 1
        return k

    def phase_fourier(l, last, ada_next):
        names = (["cst%d" % i for i in range(NCST)] + ["YT", "mixF", "fg0", "fg1", "finT0", "finT1", "w4s", "wa0", "wa1", "YTm"])
        bs = new_phase_bufs(st_scratch, names)
        b_cst = bs[0:NCST]
        b_YT, b_mixF, b_fg, b_fin, b_w4, b_wa = bs[NCST], bs[NCST + 1], bs[NCST + 2:NCST + 4], bs[NCST + 4:NCST + 6], bs[NCST + 6], bs[NCST + 7:NCST + 9]
        b_YTm = bs[NCST + 9]
        b_AB = new_phase_bufs(r1_state, ["AB%d" % j for j in range(18)])
        f1_slots = {0: load_w(0, l)}
        dma(w4s[:], w4_d[l], SL["w4"], writes=[b_w4], eng="pool")
        i2 = [0]

        def chdft(g, b, fk):
            t0, n = BLOCKS[b]
            for jl in range(n // 128):
                jt = t0 // 128 + jl
                pb = 2 + (i2[0] % 4)
                i2[0] += 1
                mm(PS[pb][:, 0:256], finT[:, fk, jl * 128:(jl + 1) * 128], cs128[:, :], True, True,
                   [b_fin[fk], B_const], [B_P[pb]])
                dve_copy(AB[:, jt, g * 256:(g + 1) * 256], PS[pb][:, 0:256], [B_P[pb]], [b_AB[jt]])

        pend = None
        for g in range(4):
            k = f1_slots[g] if g in f1_slots else load_w(g, l)
            for b, (t0, n) in enumerate(BLOCKS):
                pp = next_pp()
                proj(k, b, PS[pp], B_P[pp])
                fk = (g * 5 + b) % 2
                act(finT[:, fk, :n], PS[pp][:, :n], AF.Copy, [B_P[pp]], [b_fin[fk]])
                if pend is not None:
                    chdft(*pend)
                pend = (g, b, fk)
        chdft(*pend)
        ada_todo = list(range(24)) if ada_next is not None else []

        def next_stage():
            k_ = cnt["cst"] % NCST
            cnt["cst"] += 1
            return (wac[k_][:], b_cst[k_], SL["cst%d" % k_])

        def xb(lst, t0, n):
            return [lst[i] for i in sorted({t0 // 512, (t0 + n - 1) // 512})]

        def f3(Ysrc, b_Y, ranges, sidx):
            ntot = sum(r[1] for r in ranges)
            for oc in range(4):
                k = load_w(4 + oc, l)
                fk = oc % 2
                gb, zb = fk, 6 + fk
                for (t0, n, c0) in ranges:
                    for kc in range(8):
                        mm(PS[gb][:, c0:c0 + n], wst[:, k, kc, :], xnT[:, kc, t0:t0 + n], kc == 0, kc == 7,
                           [B_wst[k]] + xb(B_xn, t0, n), [B_P[gb]])
                act(fg[:, fk, :ntot], PS[gb][:, :ntot], AF.Silu, [B_P[gb]], [b_fg[fk]])
                for k4 in range(4):
                    mm(PS[zb][:, :ntot], w4s[:, k4, oc * 128:(oc + 1) * 128], Ysrc[:, k4, :ntot], k4 == 0, k4 == 3,
                       [b_w4, b_Y], [B_P[zb]])
                dve_tt(mixF[:, oc, :ntot], PS[zb][:, :ntot], fg[:, fk, :ntot], ALU.mult, [B_P[zb], b_fg[fk]], [b_mixF])
                if ada_todo:
                    ada_piece(ada_next, ada_todo.pop(0), b_wa, bank=2, stage=next_stage())
            banks = (0, 1, 6, 7)
            ring = [(wos[:, j_], B_wos[j_], SL["wos%d" % j_]) for j_ in range(NWOS)] + \
                   [(wosx[:, j_], b_wa[j_], SL["wa%d" % j_]) for j_ in range(2)] + \
                   [(wosy[:, j_], b_fin[j_], SL["a%d" % j_]) for j_ in range(2)]
            for dc in range(8):
                wt, bw, sl = ring[cnt["wosF"] % len(ring)]
                cnt["wosF"] += 1
                dma(wt, wo_d[l, 0, dc], sl, writes=[bw], eng="pool")
                pb = banks[dc % 4]
                for k4 in range(4):
                    mm(PS[pb][:, :ntot], wt[:, k4, :], mixF[:, k4, :ntot], k4 == 0, k4 == 3, [bw, b_mixF], [B_P[pb]])
                g_ap = modT[:, l * 48 + (16 + dc) * 2 + sidx:l * 48 + (16 + dc) * 2 + sidx + 1]
                for (t0, n, c0) in ranges:
                    dve_stt(XT[:, dc, t0:t0 + n], PS[pb][:, c0:c0 + n], g_ap, XT[:, dc, t0:t0 + n], ALU.mult, ALU.add,
                            [B_P[pb], B_mod[l]] + xb(B_XT, t0, n), xb(B_XT, t0, n))
                if ada_todo and dc in (2, 5):
                    ada_piece(ada_next, ada_todo.pop(0), b_wa, bank=2, stage=next_stage())

        for c in range(4):
            for jt in range(16):
                mm(PS[7][:, c:c + 1], AB[:, jt, c * 256:c * 256 + 128], c0v[:, 0:1], jt == 0, jt == 15,
                   [b_AB[jt], B_misc], [B_P[7]])
        b_y0 = Buf("y0")
        dve_copy(y0[:, 0:4], PS[7][:, 0:4], [B_P[7]], [b_y0])
        Qs = (Qs0, Qs1)
        b_Qs = ([b_fg[0], b_fg[1]], [b_fin[0], b_fin[1]])
        for u in range(2):
            kc0 = 1 + 512 * u
            for jt in range(16):
                k = cnt["cst"] % NCST
                cnt["cst"] += 1
                dma(cst[:, k], cs_d[jt, :, :, kc0:kc0 + 512], SL["cst%d" % k], writes=[b_cst[k]])
                for c in range(4):
                    mm(PS[2 * c][:, :], AB[:, jt, c * 256:c * 256 + 128], cst[:, k, 0, :], jt == 0, jt == 15,
                       [b_AB[jt], b_cst[k]], [B_P[2 * c]])
                    mm(PS[2 * c + 1][:, :], AB[:, jt, c * 256 + 128:c * 256 + 256], cst[:, k, 1, :], jt == 0, jt == 15,
                       [b_AB[jt], b_cst[k]], [B_P[2 * c + 1]])
            for c in range(4):
                q_, bq = Qs[c % 2], b_Qs[c % 2]
                act(q_[:, :], PS[2 * c + 1][:, :], AF.Copy, [B_P[2 * c + 1]], bq)
                dve_tt(YT[:, c, :], PS[2 * c][:, :], q_[:, :], ALU.add, [B_P[2 * c]] + bq, [b_YT])
                dve_tt(YTm[:, c, :], PS[2 * c][:, ::-1], q_[:, ::-1], ALU.subtract, [B_P[2 * c]] + bq, [b_YTm])
            if u == 0:
                f3(YT, b_YT, [(1, 512, 0)], 0)
                f3(YTm, b_YTm, [(1536, 512, 0)], 0)
            else:
                for c in range(4):
                    dve_copy(YT[:, c, 511:512], y0[:, c:c + 1], [b_y0], [b_YT])
                f3(YT, b_YT, [(513, 511, 0), (0, 1, 511)], 0)
                f3(YTm, b_YTm, [(1024, 512, 0)], 0)
        if not last:
            t0, n = BLOCKS[4]
            for ji, jt in enumerate([16, 17]):
                k = cnt["cst"] % NCST
                cnt["cst"] += 1
                dma(cst[:, k, :, 0:CTX], csc_d[:, ji, :, :], SL["cst%d" % k], writes=[b_cst[k]])
                for c in range(4):
                    mm(PS[2 + c][:, :n], AB[:, jt, c * 256:c * 256 + 128], cst[:, k, 0, :n], ji == 0, False,
                       [b_AB[jt], b_cst[k]], [B_P[2 + c]])
                    mm(PS[2 + c][:, :n], AB[:, jt, c * 256 + 128:c * 256 + 256], cst[:, k, 1, :n], False, ji == 1,
                       [b_AB[jt], b_cst[k]], [B_P[2 + c]])
            for c in range(4):
                if c % 2 == 0:
                    act(YT[:, c, :n], PS[2 + c][:, :n], AF.Copy, [B_P[2 + c]], [b_YT])
                else:
                    dve_copy(YT[:, c, :n], PS[2 + c][:, :n], [B_P[2 + c]], [b_YT])
            f3(YT, b_YT, [(t0, n, 0)], 1)
        while ada_todo:
            ada_piece(ada_next, ada_todo.pop(0), b_wa, bank=2, stage=next_stage())
        if ada_next is not None:
            ada_finish(ada_next)

    def out_proj(l, half, b, mix, b_mix, sidx, banks, dcs=range(8)):
        t0, n = BLOCKS[b]
        for dc in dcs:
            k = cnt["wos"] % NWOS
            cnt["wos"] += 1
            dma(wos[:, k], wo_d[l, half, dc], SL["wos%d" % k], writes=[B_wos[k]], eng="pool")
            pb = banks[dc % len(banks)]
            ps, bps = PS[pb], [B_P[pb]]
            for k4 in range(4):
                mm(ps[:, :n], wos[:, k, k4, :], mix[:, k4, :n], k4 == 0, k4 == 3, [B_wos[k], b_mix], bps)
            g_ap = modT[:, l * 48 + (16 + dc) * 2 + sidx:l * 48 + (16 + dc) * 2 + sidx + 1]
            dve_stt(XT[:, dc, t0:t0 + n], ps[:, :n], g_ap, XT[:, dc, t0:t0 + n], ALU.mult, ALU.add,
                    bps + [B_XT[b], B_mod[l]], [B_XT[b]])

    def qk_norm(ps, b_ps, n, gvec, out_ap, b_out, b_sq, b_tmpA, b_rstd, sqslot):
        act(sq[:, sqslot, :n], ps[:, :n], AF.Square, [b_ps], [b_sq])
        mm(PS[7][:, :n], bd[:], sq[:, sqslot, :n], True, True, [b_sq, B_misc], [B_P[7]])
        rstd_from_ss(PS[7], n, 1.0 / 64, B_P[7], b_tmpA, b_rstd)
        dve_stt(out_ap, ps[:, :n], gvec, rstd[:, :n], ALU.mult, ALU.mult, [b_ps, b_rstd, B_const, B_qg8], [b_out])

    def phase_attn(l, last, nxt):
        names = ["sq0", "sq1", "tmpA", "rstd", "wv0", "wv1", "wv2", "wv3"]
        bs = new_phase_bufs(st_scratch, names)
        b_sq, b_tmpA, b_rstd, b_wv = bs[0:2], bs[2], bs[3], bs[4:8]
        r1b = new_phase_bufs(r1_state, ["kT%d" % b for b in range(5)] + ["Vp%d" % j for j in range(18)])
        b_kT, b_Vp = r1b[0:5], r1b[5:]
        k_slots = [load_w(12 + p_, l) for p_ in range(4)]
        for fcv in range(4):
            dma(wv[:, :, fcv * 128:(fcv + 1) * 128], win_d[l, 16 + fcv], SL["wv%d" % fcv], writes=[b_wv[fcv]], eng="pool")
        for jt in range(18):
            s.op("dve", (lambda jt_: (lambda e: e.memset(Vp[:, jt_, :, 64:128], 1.0)))(jt), writes=[b_Vp[jt]])
        i = 0
        vt = [0]

        def v_tile():
            jt = vt[0]
            if jt >= 18:
                return
            vt[0] += 1
            bb = min(jt // 4, 4)
            pb = 4 + jt % 2
            for kc in range(8):
                mm(PS[pb][:, :], xnT[:, kc, jt * 128:(jt + 1) * 128], wv[:, kc, :], kc == 0, kc == 7,
                   [B_xn[bb]] + b_wv, [B_P[pb]])
            src = PS[pb][:, :].rearrange("p (a t d) -> p a t d", t=2, d=64)
            act(Vp[:, jt, :, 0:64], src[:, :, 0, :], AF.Copy, [B_P[pb]], [b_Vp[jt]])
            dve_copy(Vp[:, jt, :, 128:192], src[:, :, 1, :], [B_P[pb]], [b_Vp[jt]])

        def k_chain(p, b, pp, ii):
            t0, n = BLOCKS[b]
            sb_ = 6 + ii % 2
            mm(PS[sb_][:, :n], bd[:], sq[:, ii % 2, :n], True, True, [b_sq[ii % 2], B_misc], [B_P[sb_]])
            rstd_from_ss(PS[sb_], n, 1.0 / 64, B_P[sb_], b_tmpA, b_rstd)
            dve_stt(kT[:, p, t0:t0 + n], PS[pp][:, :n], kg[:, l:l + 1], rstd[:, :n], ALU.mult, ALU.mult,
                    [B_P[pp], b_rstd, B_const], [b_kT[b]])

        pend = None
        for p in range(4):
            k = k_slots[p]
            for b, (t0, n) in enumerate(BLOCKS):
                pp = i % 4
                proj(k, b, PS[pp], B_P[pp])
                act(sq[:, i % 2, :n], PS[pp][:, :n], AF.Square, [B_P[pp]], [b_sq[i % 2]])
                if pend is not None:
                    k_chain(*pend)
                pend = (p, b, pp, i)
                i += 1
                if i >= 3:
                    v_tile()
        k_chain(*pend)
        while vt[0] < 18:
            v_tile()
        if stop_after == "a1":
            return
        names = ["sq0", "sq1", "tmpA", "rstd", "qz", "ngT", "mixA", "PT0a", "PT0b", "PT1a", "PT1b", "tmpB2"]
        bs = new_phase_bufs(st_scratch, names)
        b_sq, b_tmpA, b_rstd, b_qz, b_ngT, b_mixA = bs[0:2], bs[2], bs[3], bs[4], bs[5], bs[6]
        b_PT = (bs[7:9], bs[9:11])
        b_tmpB2 = bs[11]
        nrm = {"i": 0}
        b_lnd, b_wgt = b_tmpA, b_rstd

        b_rstdN = Buf("rstdN")
        b_rstdN.r = list(b_tmpB2.r)

        def norm_head(bb):
            t0_, n_ = BLOCKS[bb]
            for kc in range(8):
                k_ = nrm["i"] % 2
                nrm["i"] += 1
                dve_tt(sq[:, k_, :n_], XT[:, kc, t0_:t0_ + n_], XT[:, kc, t0_:t0_ + n_], ALU.mult, [B_XT[bb]], [b_sq[k_]])
                mm(PS[7][:, :n_], ones[:], sq[:, k_, :n_], kc == 0, kc == 7, [b_sq[k_], B_misc], [B_P[7]])
            act(rstdN[:, :n_], PS[7][:, :n_], AF.Ln, [B_P[7], B_misc], [b_rstdN], bias=epsT[:, 0:1], scale=1.0 / DM)
            act(rstdN[:, :n_], rstdN[:, :n_], AF.Exp, [b_rstdN], [b_rstdN], scale=-0.5)

        def norm_tail(bb, kc, use_act=False):
            tmp_, b_tmp_ = (tmpB2, b_tmpB2) if (not use_act or kc % 2 == 0) else (tmpA, b_tmpA)
            t0_, n_ = BLOCKS[bb]
            sx = 0 if bb < 4 else 1
            g_ap = gsT[:, nxt * 16 + kc * 2 + sx:nxt * 16 + kc * 2 + sx + 1]
            sh_ap = modT[:, nxt * 48 + kc * 2 + sx:nxt * 48 + kc * 2 + sx + 1]
            dve_stt(tmp_[:, :n_], XT[:, kc, t0_:t0_ + n_], g_ap, rstdN[:, :n_], ALU.mult, ALU.mult,
                    [B_XT[bb], b_rstdN, B_mod[nxt]], [b_tmp_])
            if use_act:
                act(xnT[:, kc, t0_:t0_ + n_], tmp_[:, :n_], AF.Identity, [b_tmp_, B_mod[nxt]], [B_xn[bb]], bias=sh_ap)
            else:
                dve_ts(xnT[:, kc, t0_:t0_ + n_], tmp_[:, :n_], sh_ap, ALU.add, [b_tmp_, B_mod[nxt]], [B_xn[bb]])

        s.op("dve", lambda e: e.memset(qz[64:128, :, :, 0, :], 0.0), writes=[b_qz])
        s.op("dve", lambda e: e.memset(qz[0:64, :, :, 1, :], 0.0), writes=[b_qz])
        nblk = 4 if last else 5
        SB = ((PS[2], PS[3], B_P[2], B_P[3]), (PS[4], PS[5], B_P[4], B_P[5]))
        ic = {"v": i}

        def q_part(bb):
            t0_, n_ = BLOCKS[bb]
            ntile_ = n_ // 128

            def q_chain(p, pp, ii):
                mm(PS[7][:, :n_], bd[:], sq[:, ii % 2, :n_], True, True, [b_sq[ii % 2], B_misc], [B_P[7]])
                rstd_from_ss(PS[7], n_, 1.0 / 64, B_P[7], b_tmpA, b_rstd)
                for hh in range(2):
                    pr = slice(hh * 64, hh * 64 + 64)
                    dve_stt(qz[pr, p, 0:ntile_, hh, :], PS[pp][pr, :n_].rearrange("p (t q) -> p t q", q=128),
                            qg8[pr, l:l + 1], rstd[pr, :n_].rearrange("p (t q) -> p t q", q=128), ALU.mult, ALU.mult,
                            [B_P[pp], b_rstd, B_qg8], [b_qz])

            pend = None
            for p in range(4):
                pp = 2 + p
                k = load_w(8 + p, l)
                proj(k, bb, PS[pp], B_P[pp])
                act(sq[:, ic["v"] % 2, :n_], PS[pp][:, :n_], AF.Square, [B_P[pp]], [b_sq[ic["v"] % 2]])
                if pend is not None:
                    q_chain(*pend)
                pend = (p, pp, ic["v"])
                ic["v"] += 1
            q_chain(*pend)

        def ng_part(bb, ps_=range(4)):
            t0_, n_ = BLOCKS[bb]
            for p in ps_:
                pp = 2 + p
                k = load_w(20 + p, l)
                proj(k, bb, PS[pp], B_P[pp])
                act(ngT[:, p, :n_], PS[pp][:, :n_], AF.Silu, [B_P[pp]], [b_ngT])

        q_part(0)
        ng_part(0)
        for b in range(nblk):
            t0, n = BLOCKS[b]
            ntile = n // 128
            sidx = 0 if b < 4 else 1
            tail_todo = []
            if nxt is not None and b > 0:
                norm_head(b - 1)
                tail_todo = [(b - 1, kc) for kc in range(8)]
            def chunks_of(t):
                if b < 4:
                    return [("l", c) for c in tile_chunks(t)] + [("c", 16), ("c", 17)]
                return [("c", 16), ("c", 17)]

            steps = []
            kbs = {}
            for p in range(4):
                for tl in range(ntile):
                    t = t0 // 128 + tl
                    ch = chunks_of(t)
                    groups = [ch[0:4], ch[4:]] if len(ch) > 4 else [ch]
                    for gi, grp_ in enumerate(groups):
                        steps.append((p, t, tl, grp_, gi == 0, gi == len(groups) - 1))

            def emit_qk(si):
                p, t, tl, grp_, first, lastg = steps[si]
                if b < 4 and p not in kbs:
                    kb = cnt["bmt"] % 2
                    cnt["bmt"] += 1
                    dma(bmt[:, kb], bm_d[l, p], SL["bmt%d" % kb], writes=[B_bmt[kb]], eng="pool")
                    kbs[p] = kb
                par = si % 2
                banks = SB[par]
                for ci, (kind, c) in enumerate(grp_):
                    ps, bps = banks[ci // 2], banks[2 + ci // 2]
                    cc = (ci % 2) * 256
                    mm(ps[:, cc:cc + 256], kT[:, p, c * 128:(c + 1) * 128], qz[:, p, tl, :, :], True,
                       kind == "c", [b_kT[min(c // 4, 4)], b_qz], [bps])
                    if kind == "l":
                        v = CHUNK_VAR[(t, c)]
                        mm(ps[:, cc:cc + 256], ident[:], bmt[:, kbs[p], v, :], False, True,
                           [B_const, B_bmt[kbs[p]]], [bps])
                ncol = len(grp_) * 256
                base = 1024 + par * 1024
                act(PT[:, par, 0:ncol], psall[:, base:base + ncol], AF.Exp, [banks[2], banks[3]],
                    [b_PT[par][0], b_PT[par][1]])

            def emit_pv(si):
                p, t, tl, grp_, first, lastg = steps[si]
                par = si % 2
                ob = (6, 7) if p % 2 == 0 else (0, 1)
                for hh in range(2):
                    pso, bo = PS[ob[hh]], B_P[ob[hh]]
                    for ci, (kind, c) in enumerate(grp_):
                        mm(pso[:, tl * 128:(tl + 1) * 128], Vp[:, c, p, hh * 64:hh * 64 + 128],
                           PT[:, par, ci * 256 + hh * 128:ci * 256 + hh * 128 + 128],
                           first and ci == 0, lastg and ci == len(grp_) - 1, [b_Vp[c], b_PT[par][ci // 2]], [bo])

            def emit_norm(p):
                ob = (6, 7) if p % 2 == 0 else (0, 1)
                pA, pB, bA, bB = PS[ob[0]], PS[ob[1]], B_P[ob[0]], B_P[ob[1]]
                act(lnd[64:128, :n], pA[64:128, :n], AF.Ln, [bA], [b_lnd])
                act(lnd[0:64, :n], pB[0:64, :n], AF.Ln, [bB], [b_lnd])
                act(wgt[0:64, :n], lnd[64:128, :n], AF.Exp, [b_lnd], [b_wgt], scale=-1.0)
                act(wgt[64:128, :n], lnd[0:64, :n], AF.Exp, [b_lnd], [b_wgt], scale=-1.0)
                dve_tt(wgt[:, :n], wgt[:, :n], ngT[:, p, :n], ALU.mult, [b_wgt, b_ngT], [b_wgt])
                dve_tt(mixA[0:64, p, :n], pA[0:64, :n], wgt[0:64, :n], ALU.mult, [bA, b_wgt], [b_mixA])
                dve_tt(mixA[64:128, p, :n], pB[64:128, :n], wgt[64:128, :n], ALU.mult, [bB, b_wgt], [b_mixA])

            emit_qk(0)
            for si in range(len(steps)):
                if si + 1 < len(steps):
                    emit_qk(si + 1)
                emit_pv(si)
                if si + 1 == len(steps) or steps[si + 1][0] != steps[si][0]:
                    emit_norm(steps[si][0])
                if tail_todo and si >= 2 and si % 2 == 0:
                    norm_tail(*tail_todo.pop(0))
            while tail_todo:
                norm_tail(*tail_todo.pop(0))
            if stop_after == "a2_n":
                return
            if b + 1 < nblk:
                q_part(b + 1)
            out_proj(l, 1, b, mixA, b_mixA, sidx, (6, 7, 0, 1))
            if last and l == layers[-1] and b < 4:
                dma(oT_d[:, :, t0:t0 + n], XT[:, :, t0:t0 + n], SL["out"], reads=[B_XT[b]])
                stored.add(b)
            if b + 1 < nblk:
                ng_part(b + 1)
        if nxt is not None:
            norm_head(nblk - 1)
            for kc in range(8):
                norm_tail(nblk - 1, kc, use_act=True)

    for l in layers:
        last = (l == final_layer)
        if stop_after == "init":
            break
        if l == layers[0]:
            phase_norm(l, with_ada=True)
        if stop_after == "norm":
            break
        nxt = layers[layers.index(l) + 1] if layers.index(l) + 1 < len(layers) else None
        phase_fourier(l, last, nxt)
        if stop_after == "fourier":
            break
        phase_attn(l, last, nxt)

    dbg_map = {"modT": (modT, [128, nL * 48], F32), "gsT": (gsT, [128, nL * 16], F32),
               "xnT": (xnT, [128, 8, NT], BF16), "AB": (AB, [128, 18, 1024], BF16),
               "kT": (kT, [128, 4, NT], BF16), "Vp": (Vp, [128, 18, 4, 192], BF16),
               "mixF": (mixF, [128, 4, 512], BF16), "mixA": (mixA, [128, 4, 512], BF16),
               "YT": (YT, [128, 4, 512], BF16), "PT": (PT, [128, 2, 1024], BF16)}
    all_bufs = ([B_const, B_misc, B_cTs, B_qg8] + B_XT + B_xn + B_P + B_wst + B_wos + B_bmt + B_mod
                + st_scratch["bufs"] + r1_state["bufs"])
    for name in dbg:
        t_, shp, dt_ = dbg_map[name]
        d_ = nc.dram_tensor("dbg_" + name, shp, dt_, kind="ExternalOutput").ap()
        if len(shp) == 2:
            dma(d_[:, :], t_[:], SL["out"], reads=all_bufs)
        elif len(shp) == 3:
            dma(d_[:, :, :], t_[:], SL["out"], reads=all_bufs)
        else:
            dma(d_[:, :, :, :], t_[:], SL["out"], reads=all_bufs)

    outs = []
    for b in range(4):
        if b in stored:
            continue
        t0, n = BLOCKS[b]
        outs.append(s.dma((lambda t0_, n_: (lambda e: e.dma_start(out=oT_d[:, :, t0_:t0_ + n_], in_=XT[:, :, t0_:t0_ + n_])))(t0, n),
                          SL["out"], reads=[B_XT[b]]))
    s.finalize(final_waits=[("d", SL["out"], s.dma_count[SL["out"]])])
    st.close()
    return nc


def _dft_consts():
    j = np.arange(SEQ, dtype=np.float64)
    ang = 2.0 * np.pi * ((j[:, None] * j[None, :]) % SEQ) / SEQ
    sc = (SEQ * 128) ** -0.5
    C = np.cos(ang) * sc
    S = -np.sin(ang) * sc
    cs = np.stack([C.reshape(16, 128, SEQ), S.reshape(16, 128, SEQ)], axis=2)
    jc = np.arange(CTX, dtype=np.float64)
    angc = 2.0 * np.pi * ((jc[:, None] * jc[None, :]) % CTX) / CTX
    scc = (CTX * 128) ** -0.5
    Cc = (np.cos(angc) * scc).reshape(2, 128, CTX)
    Sc = (-np.sin(angc) * scc).reshape(2, 128, CTX)
    csc = np.stack([Cc, Sc], axis=2).transpose(1, 0, 2, 3)
    m = np.arange(128, dtype=np.float64)
    a128 = 2.0 * np.pi * ((m[:, None] * m[None, :]) % 128) / 128
    cs128 = np.concatenate([np.cos(a128), np.sin(a128)], axis=1)
    bf = ml_dtypes.bfloat16
    return (np.ascontiguousarray(cs).astype(bf), np.ascontiguousarray(csc).astype(bf),
            cs128.astype(bf), np.eye(128).astype(bf))


def _bias_tables(rel_bias):
    nl = rel_bias.shape[0]
    kc = np.arange(64)
    qc = np.arange(64)
    cs0 = np.clip(qc - 8, 0, 48)
    in_win = (kc[:, None] >= cs0[None, :]) & (kc[:, None] < cs0[None, :] + 16)
    dc_idx = np.clip(kc[:, None] - qc[None, :], -15, 15) + 15
    out = np.full((nl, 8, NV, 128, 128), NEG, dtype=np.float32)
    for (D, allowed), v in VARIANTS.items():
        for kp in range(2):
            for qp in range(2):
                if not allowed[kp][qp]:
                    continue
                dr = 2 * D + kp - qp
                blk = rel_bias[:, :, dr + 7, :][:, :, dc_idx]
                blk = np.where(in_win[None, None], blk, np.float32(NEG))
                out[:, :, v, kp * 64:(kp + 1) * 64, qp * 64:(qp + 1) * 64] = blk
    out = out.reshape(nl, 4, 2, NV, 128, 128).transpose(0, 1, 4, 3, 2, 5)
    return np.ascontiguousarray(out).reshape(nl, 4, 128, NV, 256)


def _prep_shared(inputs):
    f = np.float32
    w_ada = np.asarray(inputs["w_ada"], f)
    w_in = np.asarray(inputs["w_in"], f)
    w_four = np.asarray(inputs["w_four"], f)
    w_out = np.asarray(inputs["w_out"], f)
    nl = w_in.shape[0]
    cs, csc, cs128, ident = _dft_consts()
    sh = {
        "wada": np.ascontiguousarray(w_ada.reshape(nl, 8, 128, 24, 128).transpose(0, 3, 2, 1, 4)),
        "win": np.ascontiguousarray(w_in.reshape(nl, 8, 128, 24, 128).transpose(0, 3, 2, 1, 4)),
        "w4": np.ascontiguousarray(w_four.reshape(nl, 4, 128, 512).transpose(0, 2, 1, 3)),
        "wo": np.ascontiguousarray(w_out.reshape(nl, 2, 4, 128, 8, 128).transpose(0, 1, 4, 3, 2, 5)),
        "bm": _bias_tables(np.asarray(inputs["rel_bias"], f)),
        "cs": cs, "csc": csc, "cs128": cs128, "ident": ident,
    }
    b_ada = np.asarray(inputs["b_ada"], f)
    norm_g = np.asarray(inputs["norm_g"], f)
    bada = b_ada.reshape(nl, 24, 128).transpose(2, 0, 1)
    sh["bada"] = np.ascontiguousarray(np.repeat(bada[..., None], 2, axis=-1)).reshape(128, nl * 48)
    ng = norm_g.reshape(nl, 8, 128).transpose(2, 0, 1)
    sh["ng2"] = np.ascontiguousarray(np.repeat(ng[..., None], 2, axis=-1)).reshape(128, nl * 16)
    sh["qg"] = np.ascontiguousarray(np.tile(np.asarray(inputs["q_norm_g"], f), (1, 2)).T)
    sh["kg"] = np.ascontiguousarray(np.tile(np.asarray(inputs["k_norm_g"], f), (1, 2)).T)
    return sh


def _prep_core(x_b, ctx_b, c_b, c_ctx):
    f = np.float32
    xt = np.concatenate([np.asarray(x_b, f).T, np.asarray(ctx_b, f).T], axis=1)
    xT = np.ascontiguousarray(xt.reshape(8, 128, NT).transpose(1, 0, 2))
    cc = np.stack([np.asarray(c_b, f), np.asarray(c_ctx, f)], axis=-1)
    cT = np.ascontiguousarray(cc.reshape(8, 128, 2).transpose(1, 0, 2))
    return {"xT": xT, "cT": cT}


_NC_CACHE = {}


def kernel(x, c, ctx, c_ctx, norm_g, w_ada, b_ada, w_in, w_four, q_norm_g, k_norm_g, rel_bias, w_out):
    inputs = dict(x=x, c=c, ctx=ctx, c_ctx=c_ctx, norm_g=norm_g, w_ada=w_ada, b_ada=b_ada, w_in=w_in,
                  w_four=w_four, q_norm_g=q_norm_g, k_norm_g=k_norm_g, rel_bias=rel_bias, w_out=w_out)
    shared = _prep_shared(inputs)
    nb = np.asarray(x).shape[0]
    in_maps = []
    for b in range(nb):
        m = dict(shared)
        m.update(_prep_core(x[b], ctx[b], c[b], c_ctx))
        in_maps.append(m)
    if "nc" not in _NC_CACHE:
        _NC_CACHE["nc"] = build_nc()
    nc = _NC_CACHE["nc"]
    res = run_bass_kernel_spmd(nc, in_maps, core_ids=list(range(nb)))
    out = np.empty((nb, SEQ, DM), dtype=np.float32)
    for b in range(nb):
        oT = np.asarray(res.results[b]["oT"], dtype=np.float32)
        out[b] = oT.transpose(1, 0, 2).reshape(DM, SEQ).T
    return out
```

```python
import contextlib
import numpy as np
import ml_dtypes
import concourse.bass as bass
import concourse.mybir as mybir
from concourse.bass_utils import run_bass_kernel_spmd

F32, BF16 = mybir.dt.float32, mybir.dt.bfloat16
AF = mybir.ActivationFunctionType
ALU = mybir.AluOpType

DEPTH = 4
DM = 1024
SEQ = 2048
CTX = 256
NT = SEQ + CTX
BLOCKS = [(0, 512), (512, 512), (1024, 512), (1536, 512), (2048, 256)]
EPS = 1e-6
NEG = -30000.0
ENGS = ("pe", "act", "dve", "pool", "sp")

RS = [min(max(r - 4, 0), 24) for r in range(32)]


def tile_chunks(t):
    lo = RS[2 * t] // 2
    hi = (RS[2 * t + 1] + 7) // 2
    return list(range(lo, hi + 1))


def chunk_allowed(t, c):
    return tuple(
        tuple(RS[2 * t + qp] <= 2 * c + kp <= RS[2 * t + qp] + 7 for qp in (0, 1)) for kp in (0, 1)
    )


def build_variants():
    var = {}
    cmap = {}
    for t in range(16):
        for c in tile_chunks(t):
            key = (c - t, chunk_allowed(t, c))
            if key not in var:
                var[key] = len(var)
            cmap[(t, c)] = var[key]
    return var, cmap


VARIANTS, CHUNK_VAR = build_variants()
NV = len(VARIANTS)


class Buf:
    __slots__ = ("name", "w", "r")

    def __init__(self, name=""):
        self.name = name
        self.w = None
        self.r = []


class Sched:
    def __init__(self, nc, n_dma):
        self.nc = nc
        self.q = {e: [] for e in ENGS}
        self.n_dma = n_dma
        self.dma_count = [0] * n_dma

    def _deps(self, reads, writes, extra):
        waits = list(extra)
        for b in reads:
            if b.w is not None:
                waits.append(b.w)
        for b in writes:
            if b.w is not None:
                waits.append(b.w)
            waits.extend(b.r)
        return waits

    def _commit(self, tok, reads, writes):
        for b in reads:
            b.r.append(tok)
        for b in writes:
            b.w = tok
            b.r = []

    def op(self, eng, fn, reads=(), writes=(), extra=()):
        waits = self._deps(reads, writes, extra)
        tok = ("e", eng, len(self.q[eng]))
        self.q[eng].append([fn, waits, False, None])
        self._commit(tok, reads, writes)
        return tok

    def dma(self, fn, slot, reads=(), writes=(), eng="sp", extra=()):
        waits = self._deps(reads, writes, extra)
        self.dma_count[slot] += 16
        tok = ("d", slot, self.dma_count[slot])
        self.q[eng].append([fn, waits, False, slot])
        self._commit(tok, reads, writes)
        return tok

    def finalize(self, final_waits=()):
        nc = self.nc
        for e in ENGS:
            for item in self.q[e]:
                for t in item[1]:
                    if t[0] == "e":
                        self.q[t[1]][t[2]][2] = True
        for t in final_waits:
            if t[0] == "e":
                self.q[t[1]][t[2]][2] = True
        cum = {}
        for e in ENGS:
            c = 0
            arr = []
            for item in self.q[e]:
                if item[2]:
                    c += 1
                arr.append(c)
            cum[e] = arr
        with contextlib.ExitStack() as st:
            esem = {e: st.enter_context(nc.semaphore("s_" + e)) for e in ENGS}
            dsem = [st.enter_context(nc.semaphore("d_%d" % i)) for i in range(self.n_dma)]
            block = st.enter_context(nc.Block())

            def resolve(t):
                if t[0] == "e":
                    return ("e", t[1]), esem[t[1]], cum[t[1]][t[2]]
                return ("d", t[1]), dsem[t[1]], t[2]

            def run(e, engobj, tail=()):
                waited = {}

                def do_waits(waits):
                    need = {}
                    for t in waits:
                        key, sem, val = resolve(t)
                        if key == ("e", "pe") and e == "pe":
                            continue
                        if waited.get(key, 0) >= val:
                            continue
                        if need.get(key, (None, 0))[1] < val:
                            need[key] = (sem, val)
                    for key, (sem, val) in need.items():
                        engobj.wait_ge(sem, val)
                        waited[key] = val

                for fn, waits, sig, dslot in self.q[e]:
                    do_waits(waits)
                    ins = fn(engobj)
                    if dslot is not None:
                        ins.then_inc(dsem[dslot], 16)
                    elif sig:
                        ins.then_inc(esem[e], 1)
                do_waits(tail)

            @block.tensor
            def _(eng):
                run("pe", eng)

            @block.scalar
            def _(eng):
                run("act", eng)

            @block.vector
            def _(eng):
                run("dve", eng)

            @block.gpsimd
            def _(eng):
                run("pool", eng)

            @block.sync
            def _(eng):
                run("sp", eng, tail=final_waits)


def build_nc(layers=tuple(range(DEPTH)), final_layer=DEPTH - 1, stop_after=None, dbg=()):
    nc = bass.Bass("TRN2", target_bir_lowering=False)
    nL = DEPTH

    def din(name, shape, dt=F32):
        return nc.dram_tensor(name, list(shape), dt, kind="ExternalInput").ap()

    xT_d = din("xT", [128, 8, NT])
    cT_d = din("cT", [128, 8, 2])
    bada_d = din("bada", [128, nL * 48])
    ng2_d = din("ng2", [128, nL * 16])
    qg_d = din("qg", [128, nL])
    kg_d = din("kg", [128, nL])
    wada_d = din("wada", [nL, 24, 128, 8, 128])
    win_d = din("win", [nL, 24, 128, 8, 128])
    w4_d = din("w4", [nL, 128, 4, 512])
    wo_d = din("wo", [nL, 2, 8, 128, 4, 128])
    bm_d = din("bm", [nL, 4, 128, NV, 256])
    cs_d = din("cs", [16, 128, 2, SEQ], BF16)
    csc_d = din("csc", [128, 2, 2, CTX], BF16)
    cs128_d = din("cs128", [128, 256], BF16)
    ident_d = din("ident", [128, 128], BF16)
    oT_d = nc.dram_tensor("oT", [128, 8, SEQ], F32, kind="ExternalOutput").ap()

    off = [16512]

    def alloc(name, shape, dt, at=None):
        nbytes = int(np.prod(shape[1:])) * (4 if dt == F32 else 2)
        nbytes = (nbytes + 31) // 32 * 32
        if at is None:
            o = off[0]
            off[0] += nbytes
        else:
            o = at
        return nc.alloc_sbuf_tensor_at(name, list(shape), dt, offset=o), o + nbytes

    NWST = 4
    NWOS = 4
    NCST = 3
    XT, _ = alloc("XT", [128, 8, NT], F32)
    xnT, _ = alloc("xnT", [128, 8, NT], BF16)
    r1 = off[0]
    AB, _ = alloc("AB", [128, 18, 1024], BF16, at=r1)
    kT, e1 = alloc("kT", [128, 4, NT], BF16, at=r1)
    Vp, e2 = alloc("Vp", [128, 18, 4, 192], BF16, at=e1)
    off[0] = e2
    wst, _ = alloc("wst", [128, NWST, 8, 128], BF16)
    wos, _ = alloc("wos", [128, NWOS, 4, 128], BF16)
    bmt, _ = alloc("bmt", [128, 2, NV, 256], BF16)
    ident, _ = alloc("ident", [128, 128], BF16)
    ones, _ = alloc("ones", [128, 128], BF16)
    bd, _ = alloc("bd", [128, 128], BF16)
    cs128, _ = alloc("cs128", [128, 256], BF16)
    cT, _ = alloc("cT", [128, 8, 2], F32)
    cTs, _ = alloc("cTs", [128, 8, 2], BF16)
    bada, _ = alloc("bada", [128, nL * 48], F32)
    ng2, _ = alloc("ng2", [128, nL * 16], F32)
    qg, _ = alloc("qg", [128, nL], F32)
    kg, _ = alloc("kg", [128, nL], F32)
    qg8, _ = alloc("qg8", [128, nL], F32)
    modT, _ = alloc("modT", [128, nL * 48], F32)
    gsT, _ = alloc("gsT", [128, nL * 16], F32)
    epsT, _ = alloc("epsT", [128, 1], F32)
    rstdN, _ = alloc("rstdN", [128, 512], F32)
    y0, _ = alloc("y0", [128, 16], BF16)
    c0v, _ = alloc("c0v", [128, 16], BF16)
    S0 = off[0]
    sq, e = alloc("sq", [128, 4, 512], BF16, at=S0)
    tmpA, e = alloc("tmpA", [128, 512], F32, at=e)
    rstd, e = alloc("rstd", [128, 512], F32, at=e)
    S1 = e
    tmpB, eN = alloc("tmpB", [128, 2, 512], F32, at=e)
    cst, e = alloc("cst", [128, NCST, 2, 512], BF16, at=S0)
    wac = [alloc("wac%d" % k_, [128, 8, 128], BF16, at=S0 + k_ * 2048)[0] for k_ in range(NCST)]
    wa0s = [alloc("wa0s%d" % k_, [128, 8, 128], BF16, at=r1 + k_ * 2048)[0] for k_ in range(8)]
    YT, e = alloc("YT", [128, 4, 512], BF16, at=e)
    YTm, e = alloc("YTm", [128, 4, 512], BF16, at=e)
    mixF, e = alloc("mixF", [128, 4, 512], BF16, at=e)
    fg_off = e
    fg, e = alloc("fg", [128, 2, 512], BF16, at=e)
    fin_off = e
    finT, e = alloc("finT", [128, 2, 512], BF16, at=e)
    Qs0, _ = alloc("Qs0", [128, 512], F32, at=fg_off)
    Qs1, _ = alloc("Qs1", [128, 512], F32, at=fin_off)
    w4s, e = alloc("w4s", [128, 4, 512], BF16, at=e)
    wa_off = e
    wa, eF = alloc("wa", [128, 1, 8, 128], BF16, at=e)
    wosx, _ = alloc("wosx", [128, 2, 4, 128], BF16, at=wa_off)
    wosy, _ = alloc("wosy", [128, 2, 4, 128], BF16, at=fin_off)
    wv, eA1 = alloc("wv", [128, 8, 512], BF16, at=S1)
    qz, e = alloc("qz", [128, 4, 4, 2, 128], BF16, at=S1)
    ngT, e = alloc("ngT", [128, 4, 512], BF16, at=e)
    mixA, e = alloc("mixA", [128, 4, 512], BF16, at=e)
    pt_off = e
    PT, eA2 = alloc("PT", [128, 2, 1024], BF16, at=e)
    wosz, _ = alloc("wosz", [128, 4, 4, 128], BF16, at=pt_off)
    lnd, wgt = tmpA, rstd
    tmpB2, _ = alloc("tmpB2", [128, 512], F32, at=S0 + 2048)
    top = max(eN, eF, eA1, eA2)
    assert top <= 229344, top

    st = contextlib.ExitStack()
    psall = st.enter_context(nc.psum_tensor("psall", [128, 4096], F32))
    PS = [psall[:, i * 512:(i + 1) * 512] for i in range(8)]

    slot_names = (["a%d" % i for i in range(8)] + ["const", "x0", "x1", "x2", "x3", "x4", "wa0", "wa1", "bmt0", "bmt1", "w4", "wv0", "wv1", "wv2", "wv3", "out"]
                  + ["wst%d" % i for i in range(NWST)] + ["wos%d" % i for i in range(NWOS)]
                  + ["cst%d" % i for i in range(NCST)])
    SL = {n: i for i, n in enumerate(slot_names)}
    s = Sched(nc, len(slot_names))

    B_XT = [Buf("XT%d" % b) for b in range(5)]
    B_xn = [Buf("xn%d" % b) for b in range(5)]
    B_P = [Buf("P%d" % i) for i in range(8)]
    B_const = Buf("const")
    B_wst = [Buf("wst%d" % i) for i in range(NWST)]
    B_wos = [Buf("wos%d" % i) for i in range(NWOS)]
    B_bmt = [Buf("bmt0"), Buf("bmt1")]
    B_mod = [Buf("mod%d" % l) for l in range(nL)]
    B_cTs = Buf("cTs")
    B_qg8 = Buf("qg8")
    st_scratch = {"bufs": []}
    r1_state = {"bufs": []}

    def new_phase_bufs(state, names):
        toks = []
        for b in state["bufs"]:
            if b.w is not None:
                toks.append(b.w)
            toks.extend(b.r)
        out = []
        for n in names:
            b = Buf(n)
            b.r = list(toks)
            out.append(b)
        state["bufs"] = out
        return out

    cnt = {"wst": 0, "wos": 0, "bmt": 0, "cst": 0, "wa": 0, "pp": 0, "wosF": 0}

    def mm(out, lhsT, rhs, start, stop, reads, writes):
        return s.op("pe", lambda e: e.matmul(out, lhsT=lhsT, rhs=rhs, start=start, stop=stop), reads, writes)

    def act(out, in_, func, reads, writes, bias=None, scale=None):
        kw = {}
        if bias is not None:
            kw["bias"] = bias
        if scale is not None:
            kw["scale"] = scale
        return s.op("act", lambda e: e.activation(out=out, in_=in_, func=func, **kw), reads, writes)

    def dve_tt(out, in0, in1, op, reads, writes):
        return s.op("dve", lambda e: e.tensor_tensor(out=out, in0=in0, in1=in1, op=op), reads, writes)

    def dve_stt(out, in0, scalar, in1, op0, op1, reads, writes):
        return s.op("dve", lambda e: e.scalar_tensor_tensor(out=out, in0=in0, scalar=scalar, in1=in1,
                                                            op0=op0, op1=op1), reads, writes)

    def dve_ts(out, in0, scalar1, op0, reads, writes):
        return s.op("dve", lambda e: e.tensor_scalar(out=out, in0=in0, scalar1=scalar1, scalar2=None, op0=op0),
                    reads, writes)

    def dve_copy(out, in_, reads, writes):
        return s.op("dve", lambda e: e.tensor_copy(out=out, in_=in_), reads, writes)

    def dma(out, in_, slot, reads=(), writes=(), eng="sp"):
        return s.dma(lambda e: e.dma_start(out=out, in_=in_), slot, reads=reads, writes=writes, eng=eng)

    def load_w(fc, l):
        k = cnt["wst"] % NWST
        cnt["wst"] += 1
        dma(wst[:, k], win_d[l, fc], SL["wst%d" % k], writes=[B_wst[k]], eng="pool")
        return k

    cd = SL["const"]
    for dst, src in ((ident[:], ident_d[:, :]), (cs128[:], cs128_d[:, :]),
                     (cT[:], cT_d[:, :, :]), (bada[:], bada_d[:, :]), (ng2[:], ng2_d[:, :]),
                     (qg[:], qg_d[:, :]), (kg[:], kg_d[:, :])):
        dma(dst, src, cd)
    const_tok = ("d", cd, s.dma_count[cd])
    B_const.w = const_tok
    for b, (t0, n) in enumerate(BLOCKS):
        dma(XT[:, :, t0:t0 + n], xT_d[:, :, t0:t0 + n], SL["x%d" % b], writes=[B_XT[b]])
    B_misc = Buf("misc")
    s.op("pool", lambda e: e.memset(ones[:], 1.0), writes=[B_misc])
    s.op("pool", lambda e: e.memset(bd[:], 0.0), writes=[B_misc])
    s.op("pool", lambda e: e.memset(bd[0:64, 0:64], 1.0), writes=[B_misc])
    s.op("pool", lambda e: e.memset(bd[64:128, 64:128], 1.0), writes=[B_misc])
    s.op("pool", lambda e: e.memset(epsT[:], EPS), writes=[B_misc])
    s.op("pool", lambda e: e.memset(c0v[:], 1.0 / 512.0), writes=[B_misc])
    act(cTs[:], cT[:], AF.Silu, [B_const], [B_cTs])
    dve_ts(qg8[:], qg[:], 0.125, ALU.mult, [B_const], [B_qg8])

    ada_state = {}
    stored = set()

    def ada_piece(l, fcg, b_wa, bank=0, stage=None):
        if stage is None:
            wt, bw, sl = wa[:, 0], b_wa[0], SL["wa0"]
        else:
            wt, bw, sl = stage
        dma(wt, wada_d[l, fcg], sl, writes=[bw], eng="pool")
        for kc in range(8):
            mm(PS[bank][:, 0:2], wt[:, kc, :], cTs[:, kc, :], kc == 0, kc == 7, [bw, B_cTs], [B_P[bank]])
        c0 = l * 48 + fcg * 2
        dve_tt(modT[:, c0:c0 + 2], PS[bank][:, 0:2], bada[:, c0:c0 + 2], ALU.add, [B_P[bank], B_const], [B_mod[l]])

    def ada_finish(l):
        dve_ts(gsT[:, l * 16:(l + 1) * 16], modT[:, l * 48 + 16:l * 48 + 32], 1.0, ALU.add, [B_mod[l]], [B_mod[l]])
        dve_tt(gsT[:, l * 16:(l + 1) * 16], gsT[:, l * 16:(l + 1) * 16], ng2[:, l * 16:(l + 1) * 16], ALU.mult,
               [B_mod[l], B_const], [B_mod[l]])

    def phase_ada_standalone(l):
        b_wa = new_phase_bufs(st_scratch, ["wa0", "wa1"])
        for fcg in range(24):
            ada_piece(l, fcg, b_wa)
        ada_finish(l)

    def rstd_from_ss(ps, n, inv_n, b_ps, b_tmpA, b_rstd):
        act(tmpA[:, :n], ps[:, :n], AF.Ln, [b_ps, B_misc], [b_tmpA], bias=epsT[:, 0:1], scale=inv_n)
        act(rstd[:, :n], tmpA[:, :n], AF.Exp, [b_tmpA], [b_rstd], scale=-0.5)

    def phase_norm(l, with_ada=False):
        names = ["sq0", "sq1", "sq2", "sq3", "tmpA", "rstd", "tmpB0", "tmpB1", "wa0", "wa1"]
        bs = new_phase_bufs(st_scratch, names)
        b_sq, b_tmpA, b_rstd, b_tmpB, b_wa = bs[0:4], bs[4], bs[5], bs[6:8], bs[8:10]
        ada_todo = list(range(24)) if with_ada else []
        b_a0 = [Buf("a0_%d" % k_) for k_ in range(8)]
        na = [0]

        def stage0():
            k_ = na[0] % 8
            na[0] += 1
            return (wa0s[k_][:], b_a0[k_], SL["a%d" % k_])

        i = 0
        for b, (t0, n) in enumerate(BLOCKS):
            for kc in range(8):
                k = i % 4
                i += 1
                act(sq[:, k, :n], XT[:, kc, t0:t0 + n], AF.Square, [B_XT[b]], [b_sq[k]])
                mm(PS[1 + b][:, :n], ones[:], sq[:, k, :n], kc == 0, kc == 7, [b_sq[k], B_misc], [B_P[1 + b]])
            for _ in range(5):
                if ada_todo:
                    ada_piece(l, ada_todo.pop(0), b_wa, stage=stage0())
        while ada_todo:
            ada_piece(l, ada_todo.pop(0), b_wa, stage=stage0())
        r1_state["bufs"] = r1_state["bufs"] + b_a0
        if with_ada:
            ada_finish(l)
        for b, (t0, n) in enumerate(BLOCKS):
            sidx = 0 if b < 4 else 1
            rstd_from_ss(PS[1 + b], n, 1.0 / DM, B_P[1 + b], b_tmpA, b_rstd)
            for kc in range(8):
                k = kc % 2
                g_ap = gsT[:, l * 16 + kc * 2 + sidx:l * 16 + kc * 2 + sidx + 1]
                sh_ap = modT[:, l * 48 + kc * 2 + sidx:l * 48 + kc * 2 + sidx + 1]
                dve_stt(tmpB[:, k, :n], XT[:, kc, t0:t0 + n], g_ap, rstd[:, :n], ALU.mult, ALU.mult,
                        [B_XT[b], b_rstd, B_mod[l]], [b_tmpB[k]])
                act(xnT[:, kc, t0:t0 + n], tmpB[:, k, :n], AF.Identity, [b_tmpB[k], B_mod[l]], [B_xn[b]], bias=sh_ap)

    def proj(fc_slot, b, ps, b_ps):
        t0, n = BLOCKS[b]
        for kc in range(8):
            mm(ps[:, :n], wst[:, fc_slot, kc, :], xnT[:, kc, t0:t0 + n], kc == 0, kc == 7,
               [B_wst[fc_slot], B_xn[b]], [b_ps])

    def next_pp():
        k = cnt["pp"] % 2
        cnt["pp"] += 1
        return k

    def phase_fourier(l, last, ada_next):
        names = (["cst%d" % i for i in range(NCST)] + ["YT", "mixF", "fg0", "fg1", "finT0", "finT1", "w4s", "wa0", "wa1", "YTm"])
        bs = new_phase_bufs(st_scratch, names)
        b_cst = bs[0:NCST]
        b_YT, b_mixF, b_fg, b_fin, b_w4, b_wa = bs[NCST], bs[NCST + 1], bs[NCST + 2:NCST + 4], bs[NCST + 4:NCST + 6], bs[NCST + 6], bs[NCST + 7:NCST + 9]
        b_YTm = bs[NCST + 9]
        b_AB = new_phase_bufs(r1_state, ["AB%d" % j for j in range(18)])
        f1_slots = {0: load_w(0, l)}
        dma(w4s[:], w4_d[l], SL["w4"], writes=[b_w4], eng="pool")
        i2 = [0]

        def chdft(g, b, fk):
            t0, n = BLOCKS[b]
            for jl in range(n // 128):
                jt = t0 // 128 + jl
                pb = 2 + (i2[0] % 4)
                i2[0] += 1
                mm(PS[pb][:, 0:256], finT[:, fk, jl * 128:(jl + 1) * 128], cs128[:, :], True, True,
                   [b_fin[fk], B_const], [B_P[pb]])
                dve_copy(AB[:, jt, g * 256:(g + 1) * 256], PS[pb][:, 0:256], [B_P[pb]], [b_AB[jt]])

        pend = None
        for g in range(4):
            k = f1_slots[g] if g in f1_slots else load_w(g, l)
            for b, (t0, n) in enumerate(BLOCKS):
                pp = next_pp()
                proj(k, b, PS[pp], B_P[pp])
                fk = (g * 5 + b) % 2
                act(finT[:, fk, :n], PS[pp][:, :n], AF.Copy, [B_P[pp]], [b_fin[fk]])
                if pend is not None:
                    chdft(*pend)
                pend = (g, b, fk)
        chdft(*pend)
        ada_todo = list(range(24)) if ada_next is not None else []

        def next_stage():
            k_ = cnt["cst"] % NCST
            cnt["cst"] += 1
            return (wac[k_][:], b_cst[k_], SL["cst%d" % k_])

        def xb(lst, t0, n):
            return [lst[i] for i in sorted({t0 // 512, (t0 + n - 1) // 512})]

        def f3(Ysrc, b_Y, ranges, sidx):
            ntot = sum(r[1] for r in ranges)
            for oc in range(4):
                k = load_w(4 + oc, l)
                fk = oc % 2
                gb, zb = fk, 6 + fk
                for (t0, n, c0) in ranges:
                    for kc in range(8):
                        mm(PS[gb][:, c0:c0 + n], wst[:, k, kc, :], xnT[:, kc, t0:t0 + n], kc == 0, kc == 7,
                           [B_wst[k]] + xb(B_xn, t0, n), [B_P[gb]])
                act(fg[:, fk, :ntot], PS[gb][:, :ntot], AF.Silu, [B_P[gb]], [b_fg[fk]])
                for k4 in range(4):
                    mm(PS[zb][:, :ntot], w4s[:, k4, oc * 128:(oc + 1) * 128], Ysrc[:, k4, :ntot], k4 == 0, k4 == 3,
                       [b_w4, b_Y], [B_P[zb]])
                dve_tt(mixF[:, oc, :ntot], PS[zb][:, :ntot], fg[:, fk, :ntot], ALU.mult, [B_P[zb], b_fg[fk]], [b_mixF])
                if ada_todo:
                    ada_piece(ada_next, ada_todo.pop(0), b_wa, bank=2, stage=next_stage())
            banks = (0, 1, 6, 7)
            ring = [(wos[:, j_], B_wos[j_], SL["wos%d" % j_]) for j_ in range(NWOS)] + \
                   [(wosx[:, j_], b_wa[j_], SL["wa%d" % j_]) for j_ in range(2)] + \
                   [(wosy[:, j_], b_fin[j_], SL["a%d" % j_]) for j_ in range(2)]
            for dc in range(8):
                wt, bw, sl = ring[cnt["wosF"] % len(ring)]
                cnt["wosF"] += 1
                dma(wt, wo_d[l, 0, dc], sl, writes=[bw], eng="pool")
                pb = banks[dc % 4]
                for k4 in range(4):
                    mm(PS[pb][:, :ntot], wt[:, k4, :], mixF[:, k4, :ntot], k4 == 0, k4 == 3, [bw, b_mixF], [B_P[pb]])
                g_ap = modT[:, l * 48 + (16 + dc) * 2 + sidx:l * 48 + (16 + dc) * 2 + sidx + 1]
                for (t0, n, c0) in ranges:
                    dve_stt(XT[:, dc, t0:t0 + n], PS[pb][:, c0:c0 + n], g_ap, XT[:, dc, t0:t0 + n], ALU.mult, ALU.add,
                            [B_P[pb], B_mod[l]] + xb(B_XT, t0, n), xb(B_XT, t0, n))
                if ada_todo and dc in (2, 5):
                    ada_piece(ada_next, ada_todo.pop(0), b_wa, bank=2, stage=next_stage())

        for c in range(4):
            for jt in range(16):
                mm(PS[7][:, c:c + 1], AB[:, jt, c * 256:c * 256 + 128], c0v[:, 0:1], jt == 0, jt == 15,
                   [b_AB[jt], B_misc], [B_P[7]])
        b_y0 = Buf("y0")
        dve_copy(y0[:, 0:4], PS[7][:, 0:4], [B_P[7]], [b_y0])
        Qs = (Qs0, Qs1)
        b_Qs = ([b_fg[0], b_fg[1]], [b_fin[0], b_fin[1]])
        for u in range(2):
            kc0 = 1 + 512 * u
            for jt in range(16):
                k = cnt["cst"] % NCST
                cnt["cst"] += 1
                dma(cst[:, k], cs_d[jt, :, :, kc0:kc0 + 512], SL["cst%d" % k], writes=[b_cst[k]])
                for c in range(4):
                    mm(PS[2 * c][:, :], AB[:, jt, c * 256:c * 256 + 128], cst[:, k, 0, :], jt == 0, jt == 15,
                       [b_AB[jt], b_cst[k]], [B_P[2 * c]])
                    mm(PS[2 * c + 1][:, :], AB[:, jt, c * 256 + 128:c * 256 + 256], cst[:, k, 1, :], jt == 0, jt == 15,
                       [b_AB[jt], b_cst[k]], [B_P[2 * c + 1]])
            for c in range(4):
                q_, bq = Qs[c % 2], b_Qs[c % 2]
                act(q_[:, :], PS[2 * c + 1][:, :], AF.Copy, [B_P[2 * c + 1]], bq)
                dve_tt(YT[:, c, :], PS[2 * c][:, :], q_[:, :], ALU.add, [B_P[2 * c]] + bq, [b_YT])
                dve_tt(YTm[:, c, :], PS[2 * c][:, ::-1], q_[:, ::-1], ALU.subtract, [B_P[2 * c]] + bq, [b_YTm])
            if u == 0:
                f3(YT, b_YT, [(1, 512, 0)], 0)
                f3(YTm, b_YTm, [(1536, 512, 0)], 0)
            else:
                for c in range(4):
                    dve_copy(YT[:, c, 511:512], y0[:, c:c + 1], [b_y0], [b_YT])
                f3(YT, b_YT, [(513, 511, 0), (0, 1, 511)], 0)
                f3(YTm, b_YTm, [(1024, 512, 0)], 0)
        if not last:
            t0, n = BLOCKS[4]
            for ji, jt in enumerate([16, 17]):
                k = cnt["cst"] % NCST
                cnt["cst"] += 1
                dma(cst[:, k, :, 0:CTX], csc_d[:, ji, :, :], SL["cst%d" % k], writes=[b_cst[k]])
                for c in range(4):
                    mm(PS[2 + c][:, :n], AB[:, jt, c * 256:c * 256 + 128], cst[:, k, 0, :n], ji == 0, False,
                       [b_AB[jt], b_cst[k]], [B_P[2 + c]])
                    mm(PS[2 + c][:, :n], AB[:, jt, c * 256 + 128:c * 256 + 256], cst[:, k, 1, :n], False, ji == 1,
                       [b_AB[jt], b_cst[k]], [B_P[2 + c]])
            for c in range(4):
                if c % 2 == 0:
                    act(YT[:, c, :n], PS[2 + c][:, :n], AF.Copy, [B_P[2 + c]], [b_YT])
                else:
                    dve_copy(YT[:, c, :n], PS[2 + c][:, :n], [B_P[2 + c]], [b_YT])
            f3(YT, b_YT, [(t0, n, 0)], 1)
        while ada_todo:
            ada_piece(ada_next, ada_todo.pop(0), b_wa, bank=2, stage=next_stage())
        if ada_next is not None:
            ada_finish(ada_next)

    def out_proj(l, half, b, mix, b_mix, sidx, banks, dcs=range(8), ring_extra=()):
        t0, n = BLOCKS[b]
        ring = [(wos[:, j_], B_wos[j_], SL["wos%d" % j_]) for j_ in range(NWOS)] + list(ring_extra)
        for dc in dcs:
            wt, bw, sl = ring[cnt["wos"] % len(ring)]
            cnt["wos"] += 1
            dma(wt, wo_d[l, half, dc], sl, writes=[bw], eng="pool")
            pb = banks[dc % len(banks)]
            ps, bps = PS[pb], [B_P[pb]]
            for k4 in range(4):
                mm(ps[:, :n], wt[:, k4, :], mix[:, k4, :n], k4 == 0, k4 == 3, [bw, b_mix], bps)
            g_ap = modT[:, l * 48 + (16 + dc) * 2 + sidx:l * 48 + (16 + dc) * 2 + sidx + 1]
            dve_stt(XT[:, dc, t0:t0 + n], ps[:, :n], g_ap, XT[:, dc, t0:t0 + n], ALU.mult, ALU.add,
                    bps + [B_XT[b], B_mod[l]], [B_XT[b]])

    def qk_norm(ps, b_ps, n, gvec, out_ap, b_out, b_sq, b_tmpA, b_rstd, sqslot):
        act(sq[:, sqslot, :n], ps[:, :n], AF.Square, [b_ps], [b_sq])
        mm(PS[7][:, :n], bd[:], sq[:, sqslot, :n], True, True, [b_sq, B_misc], [B_P[7]])
        rstd_from_ss(PS[7], n, 1.0 / 64, B_P[7], b_tmpA, b_rstd)
        dve_stt(out_ap, ps[:, :n], gvec, rstd[:, :n], ALU.mult, ALU.mult, [b_ps, b_rstd, B_const, B_qg8], [b_out])

    def phase_attn(l, last, nxt):
        names = ["sq0", "sq1", "tmpA", "rstd", "wv0", "wv1", "wv2", "wv3"]
        bs = new_phase_bufs(st_scratch, names)
        b_sq, b_tmpA, b_rstd, b_wv = bs[0:2], bs[2], bs[3], bs[4:8]
        r1b = new_phase_bufs(r1_state, ["kT%d" % b for b in range(5)] + ["Vp%d" % j for j in range(18)])
        b_kT, b_Vp = r1b[0:5], r1b[5:]
        k_slots = [load_w(12 + p_, l) for p_ in range(4)]
        for fcv in range(4):
            dma(wv[:, :, fcv * 128:(fcv + 1) * 128], win_d[l, 16 + fcv], SL["wv%d" % fcv], writes=[b_wv[fcv]], eng="pool")
        for jt in range(18):
            s.op("dve", (lambda jt_: (lambda e: e.memset(Vp[:, jt_, :, 64:128], 1.0)))(jt), writes=[b_Vp[jt]])
        i = 0
        vt = [0]

        def v_tile():
            jt = vt[0]
            if jt >= 18:
                return
            vt[0] += 1
            bb = min(jt // 4, 4)
            pb = 4 + jt % 2
            for kc in range(8):
                mm(PS[pb][:, :], xnT[:, kc, jt * 128:(jt + 1) * 128], wv[:, kc, :], kc == 0, kc == 7,
                   [B_xn[bb]] + b_wv, [B_P[pb]])
            src = PS[pb][:, :].rearrange("p (a t d) -> p a t d", t=2, d=64)
            act(Vp[:, jt, :, 0:64], src[:, :, 0, :], AF.Copy, [B_P[pb]], [b_Vp[jt]])
            dve_copy(Vp[:, jt, :, 128:192], src[:, :, 1, :], [B_P[pb]], [b_Vp[jt]])

        def k_chain(p, b, pp, ii):
            t0, n = BLOCKS[b]
            sb_ = 6 + ii % 2
            mm(PS[sb_][:, :n], bd[:], sq[:, ii % 2, :n], True, True, [b_sq[ii % 2], B_misc], [B_P[sb_]])
            rstd_from_ss(PS[sb_], n, 1.0 / 64, B_P[sb_], b_tmpA, b_rstd)
            dve_stt(kT[:, p, t0:t0 + n], PS[pp][:, :n], kg[:, l:l + 1], rstd[:, :n], ALU.mult, ALU.mult,
                    [B_P[pp], b_rstd, B_const], [b_kT[b]])

        pend = None
        for p in range(4):
            k = k_slots[p]
            for b, (t0, n) in enumerate(BLOCKS):
                pp = i % 4
                proj(k, b, PS[pp], B_P[pp])
                act(sq[:, i % 2, :n], PS[pp][:, :n], AF.Square, [B_P[pp]], [b_sq[i % 2]])
                if pend is not None:
                    k_chain(*pend)
                pend = (p, b, pp, i)
                i += 1
                if i >= 3:
                    v_tile()
        k_chain(*pend)
        while vt[0] < 18:
            v_tile()
        if stop_after == "a1":
            return
        names = ["sq0", "sq1", "tmpA", "rstd", "qz", "ngT", "mixA", "PT0a", "PT0b", "PT1a", "PT1b", "tmpB2"]
        bs = new_phase_bufs(st_scratch, names)
        b_sq, b_tmpA, b_rstd, b_qz, b_ngT, b_mixA = bs[0:2], bs[2], bs[3], bs[4], bs[5], bs[6]
        b_PT = (bs[7:9], bs[9:11])
        b_tmpB2 = bs[11]
        nrm = {"i": 0}
        b_lnd, b_wgt = b_tmpA, b_rstd

        b_rstdN = Buf("rstdN")
        b_rstdN.r = list(b_tmpB2.r)

        def norm_head(bb):
            t0_, n_ = BLOCKS[bb]
            for kc in range(8):
                k_ = nrm["i"] % 2
                nrm["i"] += 1
                dve_tt(sq[:, k_, :n_], XT[:, kc, t0_:t0_ + n_], XT[:, kc, t0_:t0_ + n_], ALU.mult, [B_XT[bb]], [b_sq[k_]])
                mm(PS[7][:, :n_], ones[:], sq[:, k_, :n_], kc == 0, kc == 7, [b_sq[k_], B_misc], [B_P[7]])
            act(rstdN[:, :n_], PS[7][:, :n_], AF.Ln, [B_P[7], B_misc], [b_rstdN], bias=epsT[:, 0:1], scale=1.0 / DM)
            act(rstdN[:, :n_], rstdN[:, :n_], AF.Exp, [b_rstdN], [b_rstdN], scale=-0.5)

        def norm_tail(bb, kc, use_act=False):
            tmp_, b_tmp_ = (tmpB2, b_tmpB2) if (not use_act or kc % 2 == 0) else (tmpA, b_tmpA)
            t0_, n_ = BLOCKS[bb]
            sx = 0 if bb < 4 else 1
            g_ap = gsT[:, nxt * 16 + kc * 2 + sx:nxt * 16 + kc * 2 + sx + 1]
            sh_ap = modT[:, nxt * 48 + kc * 2 + sx:nxt * 48 + kc * 2 + sx + 1]
            dve_stt(tmp_[:, :n_], XT[:, kc, t0_:t0_ + n_], g_ap, rstdN[:, :n_], ALU.mult, ALU.mult,
                    [B_XT[bb], b_rstdN, B_mod[nxt]], [b_tmp_])
            if use_act:
                act(xnT[:, kc, t0_:t0_ + n_], tmp_[:, :n_], AF.Identity, [b_tmp_, B_mod[nxt]], [B_xn[bb]], bias=sh_ap)
            else:
                dve_ts(xnT[:, kc, t0_:t0_ + n_], tmp_[:, :n_], sh_ap, ALU.add, [b_tmp_, B_mod[nxt]], [B_xn[bb]])

        s.op("dve", lambda e: e.memset(qz[64:128, :, :, 0, :], 0.0), writes=[b_qz])
        s.op("dve", lambda e: e.memset(qz[0:64, :, :, 1, :], 0.0), writes=[b_qz])
        nblk = 4 if last else 5
        SB = ((PS[2], PS[3], B_P[2], B_P[3]), (PS[4], PS[5], B_P[4], B_P[5]))
        ic = {"v": i}

        def q_part(bb):
            t0_, n_ = BLOCKS[bb]
            ntile_ = n_ // 128

            def q_chain(p, pp, ii):
                mm(PS[7][:, :n_], bd[:], sq[:, ii % 2, :n_], True, True, [b_sq[ii % 2], B_misc], [B_P[7]])
                rstd_from_ss(PS[7], n_, 1.0 / 64, B_P[7], b_tmpA, b_rstd)
                for hh in range(2):
                    pr = slice(hh * 64, hh * 64 + 64)
                    dve_stt(qz[pr, p, 0:ntile_, hh, :], PS[pp][pr, :n_].rearrange("p (t q) -> p t q", q=128),
                            qg8[pr, l:l + 1], rstd[pr, :n_].rearrange("p (t q) -> p t q", q=128), ALU.mult, ALU.mult,
                            [B_P[pp], b_rstd, B_qg8], [b_qz])

            pend = None
            for p in range(4):
                pp = 2 + p
                k = load_w(8 + p, l)
                proj(k, bb, PS[pp], B_P[pp])
                act(sq[:, ic["v"] % 2, :n_], PS[pp][:, :n_], AF.Square, [B_P[pp]], [b_sq[ic["v"] % 2]])
                if pend is not None:
                    q_chain(*pend)
                pend = (p, pp, ic["v"])
                ic["v"] += 1
            q_chain(*pend)

        def ng_part(bb, ps_=range(4)):
            t0_, n_ = BLOCKS[bb]
            for p in ps_:
                pp = 2 + p
                k = load_w(20 + p, l)
                proj(k, bb, PS[pp], B_P[pp])
                act(ngT[:, p, :n_], PS[pp][:, :n_], AF.Silu, [B_P[pp]], [b_ngT])

        q_part(0)
        ng_part(0)
        for b in range(nblk):
            t0, n = BLOCKS[b]
            ntile = n // 128
            sidx = 0 if b < 4 else 1
            tail_todo = []
            if nxt is not None and b > 0:
                norm_head(b - 1)
                tail_todo = [(b - 1, kc) for kc in range(8)]
            def chunks_of(t):
                if b < 4:
                    return [("l", c) for c in tile_chunks(t)] + [("c", 16), ("c", 17)]
                return [("c", 16), ("c", 17)]

            steps = []
            kbs = {}
            for p in range(4):
                for tl in range(ntile):
                    t = t0 // 128 + tl
                    ch = chunks_of(t)
                    groups = [ch[0:4], ch[4:]] if len(ch) > 4 else [ch]
                    for gi, grp_ in enumerate(groups):
                        steps.append((p, t, tl, grp_, gi == 0, gi == len(groups) - 1))

            def emit_qk(si):
                p, t, tl, grp_, first, lastg = steps[si]
                if b < 4 and p not in kbs:
                    kb = cnt["bmt"] % 2
                    cnt["bmt"] += 1
                    dma(bmt[:, kb], bm_d[l, p], SL["bmt%d" % kb], writes=[B_bmt[kb]], eng="pool")
                    kbs[p] = kb
                par = si % 2
                banks = SB[par]
                for ci, (kind, c) in enumerate(grp_):
                    ps, bps = banks[ci // 2], banks[2 + ci // 2]
                    cc = (ci % 2) * 256
                    mm(ps[:, cc:cc + 256], kT[:, p, c * 128:(c + 1) * 128], qz[:, p, tl, :, :], True,
                       kind == "c", [b_kT[min(c // 4, 4)], b_qz], [bps])
                    if kind == "l":
                        v = CHUNK_VAR[(t, c)]
                        mm(ps[:, cc:cc + 256], ident[:], bmt[:, kbs[p], v, :], False, True,
                           [B_const, B_bmt[kbs[p]]], [bps])
                ncol = len(grp_) * 256
                base = 1024 + par * 1024
                act(PT[:, par, 0:ncol], psall[:, base:base + ncol], AF.Exp, [banks[2], banks[3]],
                    [b_PT[par][0], b_PT[par][1]])

            def emit_pv(si):
                p, t, tl, grp_, first, lastg = steps[si]
                par = si % 2
                ob = (6, 7) if p % 2 == 0 else (0, 1)
                for hh in range(2):
                    pso, bo = PS[ob[hh]], B_P[ob[hh]]
                    for ci, (kind, c) in enumerate(grp_):
                        mm(pso[:, tl * 128:(tl + 1) * 128], Vp[:, c, p, hh * 64:hh * 64 + 128],
                           PT[:, par, ci * 256 + hh * 128:ci * 256 + hh * 128 + 128],
                           first and ci == 0, lastg and ci == len(grp_) - 1, [b_Vp[c], b_PT[par][ci // 2]], [bo])

            def emit_norm(p):
                ob = (6, 7) if p % 2 == 0 else (0, 1)
                pA, pB, bA, bB = PS[ob[0]], PS[ob[1]], B_P[ob[0]], B_P[ob[1]]
                act(lnd[64:128, :n], pA[64:128, :n], AF.Ln, [bA], [b_lnd])
                act(lnd[0:64, :n], pB[0:64, :n], AF.Ln, [bB], [b_lnd])
                act(wgt[0:64, :n], lnd[64:128, :n], AF.Exp, [b_lnd], [b_wgt], scale=-1.0)
                act(wgt[64:128, :n], lnd[0:64, :n], AF.Exp, [b_lnd], [b_wgt], scale=-1.0)
                dve_tt(wgt[:, :n], wgt[:, :n], ngT[:, p, :n], ALU.mult, [b_wgt, b_ngT], [b_wgt])
                dve_tt(mixA[0:64, p, :n], pA[0:64, :n], wgt[0:64, :n], ALU.mult, [bA, b_wgt], [b_mixA])
                dve_tt(mixA[64:128, p, :n], pB[64:128, :n], wgt[64:128, :n], ALU.mult, [bB, b_wgt], [b_mixA])

            emit_qk(0)
            for si in range(len(steps)):
                if si + 1 < len(steps):
                    emit_qk(si + 1)
                emit_pv(si)
                if si + 1 == len(steps) or steps[si + 1][0] != steps[si][0]:
                    emit_norm(steps[si][0])
                if tail_todo and si >= 2 and si % 2 == 0:
                    norm_tail(*tail_todo.pop(0))
            while tail_todo:
                norm_tail(*tail_todo.pop(0))
            if stop_after == "a2_n":
                return
            if b + 1 < nblk:
                q_part(b + 1)
            out_proj(l, 1, b, mixA, b_mixA, sidx, (6, 7, 0, 1),
                     ring_extra=[(wosz[:, j_], b_PT[j_ // 2][j_ % 2], SL["a%d" % (2 + j_)]) for j_ in range(4)])
            if last and l == layers[-1] and b < 4:
                dma(oT_d[:, :, t0:t0 + n], XT[:, :, t0:t0 + n], SL["out"], reads=[B_XT[b]])
                stored.add(b)
            if b + 1 < nblk:
                ng_part(b + 1)
        if nxt is not None:
            norm_head(nblk - 1)
            for kc in range(8):
                norm_tail(nblk - 1, kc, use_act=True)

    for l in layers:
        last = (l == final_layer)
        if stop_after == "init":
            break
        if l == layers[0]:
            phase_norm(l, with_ada=True)
        if stop_after == "norm":
            break
        nxt = layers[layers.index(l) + 1] if layers.index(l) + 1 < len(layers) else None
        phase_fourier(l, last, nxt)
        if stop_after == "fourier":
            break
        phase_attn(l, last, nxt)

    dbg_map = {"modT": (modT, [128, nL * 48], F32), "gsT": (gsT, [128, nL * 16], F32),
               "xnT": (xnT, [128, 8, NT], BF16), "AB": (AB, [128, 18, 1024], BF16),
               "kT": (kT, [128, 4, NT], BF16), "Vp": (Vp, [128, 18, 4, 192], BF16),
               "mixF": (mixF, [128, 4, 512], BF16), "mixA": (mixA, [128, 4, 512], BF16),
               "YT": (YT, [128, 4, 512], BF16), "PT": (PT, [128, 2, 1024], BF16)}
    all_bufs = ([B_const, B_misc, B_cTs, B_qg8] + B_XT + B_xn + B_P + B_wst + B_wos + B_bmt + B_mod
                + st_scratch["bufs"] + r1_state["bufs"])
    for name in dbg:
        t_, shp, dt_ = dbg_map[name]
        d_ = nc.dram_tensor("dbg_" + name, shp, dt_, kind="ExternalOutput").ap()
        if len(shp) == 2:
            dma(d_[:, :], t_[:], SL["out"], reads=all_bufs)
        elif len(shp) == 3:
            dma(d_[:, :, :], t_[:], SL["out"], reads=all_bufs)
        else:
            dma(d_[:, :, :, :], t_[:], SL["out"], reads=all_bufs)

    outs = []
    for b in range(4):
        if b in stored:
            continue
        t0, n = BLOCKS[b]
        outs.append(s.dma((lambda t0_, n_: (lambda e: e.dma_start(out=oT_d[:, :, t0_:t0_ + n_], in_=XT[:, :, t0_:t0_ + n_])))(t0, n),
                          SL["out"], reads=[B_XT[b]]))
    s.finalize(final_waits=[("d", SL["out"], s.dma_count[SL["out"]])])
    st.close()
    return nc


def _dft_consts():
    j = np.arange(SEQ, dtype=np.float64)
    ang = 2.0 * np.pi * ((j[:, None] * j[None, :]) % SEQ) / SEQ
    sc = (SEQ * 128) ** -0.5
    C = np.cos(ang) * sc
    S = -np.sin(ang) * sc
    cs = np.stack([C.reshape(16, 128, SEQ), S.reshape(16, 128, SEQ)], axis=2)
    jc = np.arange(CTX, dtype=np.float64)
    angc = 2.0 * np.pi * ((jc[:, None] * jc[None, :]) % CTX) / CTX
    scc = (CTX * 128) ** -0.5
    Cc = (np.cos(angc) * scc).reshape(2, 128, CTX)
    Sc = (-np.sin(angc) * scc).reshape(2, 128, CTX)
    csc = np.stack([Cc, Sc], axis=2).transpose(1, 0, 2, 3)
    m = np.arange(128, dtype=np.float64)
    a128 = 2.0 * np.pi * ((m[:, None] * m[None, :]) % 128) / 128
    cs128 = np.concatenate([np.cos(a128), np.sin(a128)], axis=1)
    bf = ml_dtypes.bfloat16
    return (np.ascontiguousarray(cs).astype(bf), np.ascontiguousarray(csc).astype(bf),
            cs128.astype(bf), np.eye(128).astype(bf))


def _bias_tables(rel_bias):
    nl = rel_bias.shape[0]
    kc = np.arange(64)
    qc = np.arange(64)
    cs0 = np.clip(qc - 8, 0, 48)
    in_win = (kc[:, None] >= cs0[None, :]) & (kc[:, None] < cs0[None, :] + 16)
    dc_idx = np.clip(kc[:, None] - qc[None, :], -15, 15) + 15
    out = np.full((nl, 8, NV, 128, 128), NEG, dtype=np.float32)
    for (D, allowed), v in VARIANTS.items():
        for kp in range(2):
            for qp in range(2):
                if not allowed[kp][qp]:
                    continue
                dr = 2 * D + kp - qp
                blk = rel_bias[:, :, dr + 7, :][:, :, dc_idx]
                blk = np.where(in_win[None, None], blk, np.float32(NEG))
                out[:, :, v, kp * 64:(kp + 1) * 64, qp * 64:(qp + 1) * 64] = blk
    out = out.reshape(nl, 4, 2, NV, 128, 128).transpose(0, 1, 4, 3, 2, 5)
    return np.ascontiguousarray(out).reshape(nl, 4, 128, NV, 256)


def _prep_shared(inputs):
    f = np.float32
    w_ada = np.asarray(inputs["w_ada"], f)
    w_in = np.asarray(inputs["w_in"], f)
    w_four = np.asarray(inputs["w_four"], f)
    w_out = np.asarray(inputs["w_out"], f)
    nl = w_in.shape[0]
    cs, csc, cs128, ident = _dft_consts()
    sh = {
        "wada": np.ascontiguousarray(w_ada.reshape(nl, 8, 128, 24, 128).transpose(0, 3, 2, 1, 4)),
        "win": np.ascontiguousarray(w_in.reshape(nl, 8, 128, 24, 128).transpose(0, 3, 2, 1, 4)),
        "w4": np.ascontiguousarray(w_four.reshape(nl, 4, 128, 512).transpose(0, 2, 1, 3)),
        "wo": np.ascontiguousarray(w_out.reshape(nl, 2, 4, 128, 8, 128).transpose(0, 1, 4, 3, 2, 5)),
        "bm": _bias_tables(np.asarray(inputs["rel_bias"], f)),
        "cs": cs, "csc": csc, "cs128": cs128, "ident": ident,
    }
    b_ada = np.asarray(inputs["b_ada"], f)
    norm_g = np.asarray(inputs["norm_g"], f)
    bada = b_ada.reshape(nl, 24, 128).transpose(2, 0, 1)
    sh["bada"] = np.ascontiguousarray(np.repeat(bada[..., None], 2, axis=-1)).reshape(128, nl * 48)
    ng = norm_g.reshape(nl, 8, 128).transpose(2, 0, 1)
    sh["ng2"] = np.ascontiguousarray(np.repeat(ng[..., None], 2, axis=-1)).reshape(128, nl * 16)
    sh["qg"] = np.ascontiguousarray(np.tile(np.asarray(inputs["q_norm_g"], f), (1, 2)).T)
    sh["kg"] = np.ascontiguousarray(np.tile(np.asarray(inputs["k_norm_g"], f), (1, 2)).T)
    return sh


def _prep_core(x_b, ctx_b, c_b, c_ctx):
    f = np.float32
    xt = np.concatenate([np.asarray(x_b, f).T, np.asarray(ctx_b, f).T], axis=1)
    xT = np.ascontiguousarray(xt.reshape(8, 128, NT).transpose(1, 0, 2))
    cc = np.stack([np.asarray(c_b, f), np.asarray(c_ctx, f)], axis=-1)
    cT = np.ascontiguousarray(cc.reshape(8, 128, 2).transpose(1, 0, 2))
    return {"xT": xT, "cT": cT}


_NC_CACHE = {}


def kernel(x, c, ctx, c_ctx, norm_g, w_ada, b_ada, w_in, w_four, q_norm_g, k_norm_g, rel_bias, w_out):
    inputs = dict(x=x, c=c, ctx=ctx, c_ctx=c_ctx, norm_g=norm_g, w_ada=w_ada, b_ada=b_ada, w_in=w_in,
                  w_four=w_four, q_norm_g=q_norm_g, k_norm_g=k_norm_g, rel_bias=rel_bias, w_out=w_out)
    shared = _prep_shared(inputs)
    nb = np.asarray(x).shape[0]
    in_maps = []
    for b in range(nb):
        m = dict(shared)
        m.update(_prep_core(x[b], ctx[b], c[b], c_ctx))
        in_maps.append(m)
    if "nc" not in _NC_CACHE:
        _NC_CACHE["nc"] = build_nc()
    nc = _NC_CACHE["nc"]
    res = run_bass_kernel_spmd(nc, in_maps, core_ids=list(range(nb)))
    out = np.empty((nb, SEQ, DM), dtype=np.float32)
    for b in range(nb):
        oT = np.asarray(res.results[b]["oT"], dtype=np.float32)
        out[b] = oT.transpose(1, 0, 2).reshape(DM, SEQ).T
    return out
```
